# Optimizing a Trainium2 kernel written in Bass

```python
import jax, jax.numpy as jnp
from jax import lax
import numpy as np

D_MODEL = 2048
BATCH = 2
SEQ = 4096
DEPTH = 1

N_META = 16
ATTN_HEADS = 8
HEAD_DIM = 128
ATTN_WIDTH = ATTN_HEADS * HEAD_DIM
CONV_GROUPS = 8
CONV_WIDTH = 1024
CONV_K = 3
N_BRANCH = 2
D_FF = 4 * D_MODEL
BLOCK_Q = 128
EPS = 1e-6
FGATE_BIAS = 3.0
COL_SIZES = (ATTN_WIDTH, ATTN_WIDTH, ATTN_WIDTH, ATTN_HEADS,
             CONV_WIDTH, CONV_WIDTH, CONV_WIDTH, N_BRANCH * D_MODEL)
IN_COLS = 3 * ATTN_WIDTH + ATTN_HEADS + 3 * CONV_WIDTH + N_BRANCH * D_MODEL

kernel_name = "fox_shortconv_gated_hybrid_block"


def rms_norm(x, g):
    xf = x.astype(jnp.float32)
    y = xf * lax.rsqrt(jnp.mean(xf * xf, axis=-1, keepdims=True) + EPS)
    return (y * g.astype(jnp.float32)).astype(x.dtype)


def split_offsets():
    offs, acc = [], 0
    for s in COL_SIZES[:-1]:
        acc += s
        offs.append(acc)
    return offs


def fox_block(qb, cq, qpos, k, v, ck, kpos):
    scale = HEAD_DIM ** -0.5
    s = jnp.einsum('bqhd,bkhd->bhqk', qb, k).astype(jnp.float32) * scale
    s = s + cq.transpose(0, 2, 1)[:, :, :, None] - ck.transpose(0, 2, 1)[:, :, None, :]
    mask = kpos[None, :] <= qpos[:, None]
    s = jnp.where(mask[None, None], s, -jnp.inf)
    p = jax.nn.softmax(s, axis=-1)
    return jnp.einsum('bhqk,bkhd->bqhd', p.astype(v.dtype), v)


def forgetting_attention(q, k, v, log_f):
    B, L, H, Dh = q.shape
    cum = jnp.cumsum(log_f, axis=1)
    pos = jnp.arange(L, dtype=jnp.int32)
    meta_out = fox_block(q[:, :N_META], cum[:, :N_META], pos[:N_META],
                         k[:, :N_META], v[:, :N_META], cum[:, :N_META], pos[:N_META])
    nb = (L - N_META) // BLOCK_Q
    qr = q[:, N_META:].reshape(B, nb, BLOCK_Q, H, Dh).transpose(1, 0, 2, 3, 4)
    cr = cum[:, N_META:].reshape(B, nb, BLOCK_Q, H).transpose(1, 0, 2, 3)
    pr = pos[N_META:].reshape(nb, BLOCK_Q)
    real = lax.map(lambda a: fox_block(a[0], a[1], a[2], k, v, cum, pos), (qr, cr, pr))
    real = real.transpose(1, 0, 2, 3, 4).reshape(B, L - N_META, H, Dh)
    return jnp.concatenate([meta_out, real], axis=1)


def short_conv(u, w):
    L = u.shape[1]
    up = jnp.pad(u, ((0, 0), (CONV_K - 1, 0), (0, 0)))
    y = w[0] * up[:, 0:L]
    for j in range(1, CONV_K):
        y = y + w[j] * up[:, j:j + L]
    return y


def setup_inputs(seed: int = 0) -> dict:
    key = jax.random.key(seed)
    ks = jax.random.split(key, 16)
    f32 = jnp.float32
    nrm = lambda k, shape, scale: jax.random.normal(k, shape, f32) * scale
    x = jax.random.normal(ks[0], (BATCH, SEQ, D_MODEL), f32)
    meta_tokens = nrm(ks[1], (N_META, D_MODEL), 1.0)
    norm_mix = 1.0 + nrm(ks[2], (DEPTH, D_MODEL), 0.02)
    w_in = nrm(ks[3], (DEPTH, D_MODEL, IN_COLS), D_MODEL ** -0.5)
    b_fgate = FGATE_BIAS + nrm(ks[4], (DEPTH, ATTN_HEADS), 0.1)
    b_gate = nrm(ks[5], (DEPTH, N_BRANCH * D_MODEL), 0.01)
    q_norm = 1.0 + nrm(ks[6], (DEPTH, HEAD_DIM), 0.02)
    k_norm = 1.0 + nrm(ks[7], (DEPTH, HEAD_DIM), 0.02)
    conv_w = nrm(ks[8], (DEPTH, CONV_K, CONV_WIDTH), CONV_K ** -0.5)
    w_attn_out = nrm(ks[9], (DEPTH, ATTN_WIDTH, D_MODEL), ATTN_WIDTH ** -0.5)
    w_conv_out = nrm(ks[10], (DEPTH, CONV_WIDTH, D_MODEL), CONV_WIDTH ** -0.5)
    w_o = nrm(ks[11], (DEPTH, D_MODEL, D_MODEL), D_MODEL ** -0.5)
    norm_mlp = 1.0 + nrm(ks[12], (DEPTH, D_MODEL), 0.02)
    w_up = nrm(ks[13], (DEPTH, D_MODEL, D_FF), D_MODEL ** -0.5)
    w_down = nrm(ks[14], (DEPTH, D_FF, D_MODEL), D_FF ** -0.5)
    return {"x": x, "meta_tokens": meta_tokens, "norm_mix": norm_mix, "w_in": w_in,
            "b_fgate": b_fgate, "b_gate": b_gate, "q_norm": q_norm, "k_norm": k_norm,
            "conv_w": conv_w, "w_attn_out": w_attn_out, "w_conv_out": w_conv_out,
            "w_o": w_o, "norm_mlp": norm_mlp, "w_up": w_up, "w_down": w_down}


def reference(x, meta_tokens, norm_mix, w_in, b_fgate, b_gate, q_norm, k_norm,
              conv_w, w_attn_out, w_conv_out, w_o, norm_mlp, w_up, w_down):
    B = x.shape[0]
    meta = jnp.broadcast_to(meta_tokens[None].astype(x.dtype), (B, N_META, D_MODEL))
    h = jnp.concatenate([meta, x], axis=1)
    L = h.shape[1]
    offs = split_offsets()
    for layer in range(DEPTH):
        xn = rms_norm(h, norm_mix[layer])
        proj = jnp.einsum('bld,dc->blc', xn, w_in[layer])
        q, k, v, fg, cb, cc, cx, gl = jnp.split(proj, offs, axis=-1)
        q = rms_norm(q.reshape(B, L, ATTN_HEADS, HEAD_DIM), q_norm[layer])
        k = rms_norm(k.reshape(B, L, ATTN_HEADS, HEAD_DIM), k_norm[layer])
        v = v.reshape(B, L, ATTN_HEADS, HEAD_DIM)
        log_f = jax.nn.log_sigmoid(fg.astype(jnp.float32) + b_fgate[layer].astype(jnp.float32))
        a = forgetting_attention(q, k, v, log_f).reshape(B, L, ATTN_WIDTH)
        a = jnp.einsum('blc,cd->bld', a, w_attn_out[layer])
        c = cb * short_conv(cc * cx, conv_w[layer])
        c = jnp.einsum('blc,cd->bld', c, w_conv_out[layer])
        g = jax.nn.sigmoid(gl.astype(jnp.float32) + b_gate[layer].astype(jnp.float32))
        g = g.astype(h.dtype).reshape(B, L, N_BRANCH, D_MODEL)
        merged = g[:, :, 0] * a + g[:, :, 1] * c
        h = h + jnp.einsum('bld,de->ble', merged, w_o[layer])
        hn = rms_norm(h, norm_mlp[layer])
        u = jnp.square(jax.nn.relu(jnp.einsum('bld,df->blf', hn, w_up[layer])))
        h = h + jnp.einsum('blf,fd->bld', u, w_down[layer])
    return h[:, N_META:]
```

```python
import numpy as np
from contextlib import ExitStack
import concourse.bass as bass
import concourse.mybir as mybir
from concourse.bass_utils import run_bass_kernel_spmd

F32, BF16 = mybir.dt.float32, mybir.dt.bfloat16
AF = mybir.ActivationFunctionType
ALU = mybir.AluOpType

D = 2048
KC = 16
NMETA = 16
SEQ = 4096
L = SEQ + NMETA
H = 8
DH = 128
NR = 8
TOWN = NR * 128
IN_COLS = 10248
OFF_Q, OFF_K, OFF_V, OFF_FG = 0, 1024, 2048, 3072
OFF_CB, OFF_CC, OFF_CX, OFF_GL = 3080, 4104, 5128, 6152
DFF = 8192
EPS = 1e-6
SCALE = DH ** -0.5
NEG = -30000.0

C_GMIX, C_GMLP, C_GQ, C_GK, C_BFG, C_BGATE, C_CONV, C_MASKB, C_PRESEL = 0, 16, 32, 33, 34, 42, 74, 98, 102
C_EPS, C_ONE = 106, 107
NCST = 108
M_ONES, M_U, M_ID, M_TRI = 0, 128, 256, 384
NMAT = 896


class Sched:
    NDSEM = 8

    def __init__(self, nc, es):
        self.nc = nc
        self.eng = {"pe": nc.tensor, "act": nc.scalar, "dve": nc.vector, "pool": nc.gpsimd, "sp": nc.sync}
        self.sem = {e: es.enter_context(nc.semaphore("sem_" + e)) for e in self.eng}
        self.cnt = {e: 0 for e in self.eng}
        self.waited = {e: {} for e in self.eng}
        self.lastw = {}
        self.readers = {}
        self.dsem = {q: [es.enter_context(nc.semaphore("d_%s_%d" % (q, i))) for i in range(self.NDSEM)]
                     for q in ("sp", "pool")}
        self.duse = {q: [0] * self.NDSEM for q in ("sp", "pool")}
        self.dnext = {q: 0 for q in ("sp", "pool")}

    def _wait(self, e, tok):
        key, sem, val = tok
        if e == "pe" and key == "pe":
            return
        if self.waited[e].get(key, 0) >= val:
            return
        self.eng[e].wait_ge(sem, val)
        self.waited[e][key] = val

    @staticmethod
    def _expand(keys):
        out = []
        for k in keys:
            if isinstance(k, tuple) and len(k) > 0 and k[0] == "multi":
                out.extend(k[1:])
            else:
                out.append(k)
        return out

    def _deps(self, reads, writes):
        reads, writes = self._expand(reads), self._expand(writes)
        best = {}
        def add(t):
            if t[0] not in best or best[t[0]][2] < t[2]:
                best[t[0]] = t
        for k in reads:
            if k in self.lastw:
                add(self.lastw[k])
        for k in writes:
            if k in self.lastw:
                add(self.lastw[k])
            for t in self.readers.get(k, {}).values():
                add(t)
        return list(best.values())

    def _record(self, tok, reads, writes):
        reads, writes = self._expand(reads), self._expand(writes)
        for k in writes:
            self.lastw[k] = tok
            self.readers[k] = {}
        for k in reads:
            self.readers.setdefault(k, {})[tok[0]] = tok

    def op(self, e, fn, reads=(), writes=()):
        for t in self._deps(reads, writes):
            self._wait(e, t)
        ins = fn(self.eng[e])
        self.cnt[e] += 1
        ins.then_inc(self.sem[e], 1)
        tok = (e, self.sem[e], self.cnt[e])
        self._record(tok, reads, writes)
        return tok

    def dma(self, q, out, in_, reads=(), writes=()):
        for t in self._deps(reads, writes):
            self._wait(q, t)
        i = self.dnext[q]
        self.dnext[q] = (i + 1) % self.NDSEM
        sem = self.dsem[q][i]
        key = "d_%s_%d" % (q, i)
        if self.duse[q][i] > 0:
            self._wait(q, (key, sem, 16 * self.duse[q][i]))
        self.eng[q].dma_start(out=out, in_=in_).then_inc(sem, 16)
        self.duse[q][i] += 1
        tok = (key, sem, 16 * self.duse[q][i])
        self._record(tok, reads, writes)
        return tok

    def barrier(self):
        toks = [(e, self.sem[e], self.cnt[e]) for e in self.eng if self.cnt[e] > 0]
        for q in ("sp", "pool"):
            for i in range(self.NDSEM):
                if self.duse[q][i] > 0:
                    toks.append(("d_%s_%d" % (q, i), self.dsem[q][i], 16 * self.duse[q][i]))
        for e in self.eng:
            for t in toks:
                self._wait(e, t)

    def drain(self, q):
        for i in range(self.NDSEM):
            if self.duse[q][i] > 0:
                self._wait(q, ("d_%s_%d" % (q, i), self.dsem[q][i], 16 * self.duse[q][i]))


class Ring:
    def __init__(self, name, n):
        self.name, self.n, self.i = name, n, 0

    def next(self):
        i = self.i
        self.i = (i + 1) % self.n
        return i, (self.name, i)


class Bump:
    def __init__(self, nc, base):
        self.nc, self.off, self.n = nc, base, 0

    def __call__(self, name, shape, dtype, at=None):
        nbytes = int(np.prod(shape[1:])) * (2 if dtype == BF16 else 4)
        if at is None:
            at = self.off
            self.off = (at + nbytes + 31) // 32 * 32
        assert at + nbytes <= 229376, (name, at, nbytes)
        self.n += 1
        return self.nc.alloc_sbuf_tensor_at("%s_%d" % (name, self.n), list(shape), dtype, offset=at)


def build_program():
    nc = bass.Bass("TRN2", target_bir_lowering=False)
    dt = lambda name, shape, kind="ExternalInput": nc.dram_tensor(name, shape, F32, kind=kind).ap()
    xT = dt("xT", [D, L])
    xoT = dt("xoT", [D, TOWN])
    xhT = dt("xhT", [D, 16])
    w_in = dt("w_in", [D, IN_COLS])
    w_ao = dt("w_ao", [1024, D])
    w_co = dt("w_co", [1024, D])
    w_o = dt("w_o", [D, D])
    w_up = dt("w_up", [D, DFF])
    w_down = dt("w_down", [DFF, D])
    cst_d = dt("cst", [128, NCST])
    cmat_d = dt("cmat", [128, NMAT])
    yT = dt("yT", [D, TOWN], kind="ExternalOutput")

    with ExitStack() as es:
        S = Sched(nc, es)
        ps = lambda name, shape, dtype, st: st.enter_context(nc.psum_tensor(name, shape, dtype))

        G = Bump(nc, 16640)
        cst = G("cst", [128, NCST], F32)
        cmat = G("cmat", [128, NMAT], F32)
        ones_bf = G("ones_bf", [128, 128], BF16)
        id_bf = G("id_bf", [128, 128], BF16)
        tri_bf = G("tri_bf", [128, 4, 128], BF16)
        stage = G("stage", [128, 6, 512], F32)
        sq = G("sq", [128, 4, 512], BF16)
        P0 = 16640 + 23552
        assert G.off <= P0, G.off
        stage_r, sq_r = Ring("stg", 6), Ring("sq", 4)
        ones_f = cmat[:, M_ONES:M_ONES + 128]
        U_f = cmat[:, M_U:M_U + 128]
        id_f = cmat[:, M_ID:M_ID + 128]

        S.dma("sp", cst[:], cst_d, writes=["cst"])
        S.dma("sp", cmat[:], cmat_d, writes=["cmat"])
        S.op("dve", lambda e: e.memset(ones_bf[:], 1.0), writes=["ones_bf"])
        S.op("dve", lambda e: e.tensor_copy(out=id_bf[:], in_=id_f), reads=["cmat"], writes=["id_bf"])
        S.op("dve", lambda e: e.tensor_copy(out=tri_bf[:], in_=cmat[:, M_TRI:M_TRI + 512].rearrange("p (a b) -> p a b", a=4)),
             reads=["cmat"], writes=["tri_bf"])

        def cc(col):
            return cst[:, col:col + 1]

        def load_w(dst, key, dram, row0, kc, col0, ncol):
            src = dram[row0:row0 + kc * 128, col0:col0 + ncol].rearrange("(kc p) n -> p kc n", p=128)
            S.dma("pool", dst, src, writes=[key])

        def rstd_from(out_ap, out_key, ssq_ap, ssq_key, n):
            S.op("act", lambda e: e.activation(out=out_ap, in_=ssq_ap, func=AF.Ln, scale=1.0 / n, bias=cc(C_EPS)),
                 reads=[ssq_key, "cst"], writes=[out_key])
            S.op("act", lambda e: e.activation(out=out_ap, in_=out_ap, func=AF.Exp, scale=-0.5),
                 reads=[out_key], writes=[out_key])

        def rms_prep(src, T, xg, xg_key, gcol, rstd, rstd_key, ssq_ps, ssq_key):
            for kc in range(KC):
                si, sk = stage_r.next()
                S.dma("sp", stage[:, si, :T], src[kc * 128:(kc + 1) * 128, :], writes=[sk])
                qi, qk = sq_r.next()
                S.op("act", lambda e, si=si, qi=qi: e.activation(out=sq[:, qi, :T], in_=stage[:, si, :T], func=AF.Square),
                     reads=[sk], writes=[qk])
                S.op("pe", lambda e, qi=qi, kc=kc: e.matmul(ssq_ps[:, :T], lhsT=ones_bf[:], rhs=sq[:, qi, :T],
                                                            start=(kc == 0), stop=(kc == KC - 1)),
                     reads=[qk, "ones_bf"], writes=[ssq_key])
                S.op("dve", lambda e, si=si, kc=kc: e.tensor_scalar(out=xg[:, kc, :T], in0=stage[:, si, :T],
                                                                    scalar1=cc(gcol + kc), scalar2=None, op0=ALU.mult),
                     reads=[sk, "cst"], writes=[xg_key])
            rstd_from(rstd[:, :T], rstd_key, ssq_ps[:, :T], ssq_key, D)

        with ExitStack() as esb:
            pb = [ps("pb%d" % i, [128, 512], F32, esb) for i in range(8)]
            pk = [("ps", i) for i in range(8)]
            proj_r = Ring("projb", 2)
            s_r = Ring("sbank", 3)
            o_r = Ring("oslot", 2)

            M = Bump(nc, P0)
            QT = M("QT", [128, H, TOWN], BF16)
            wk = M("wk", [128, KC, 1024], BF16)
            wv = M("wv", [128, KC, 1024], BF16)
            wfg = M("wfg", [128, KC, 8], BF16)
            xg = M("xg", [128, KC, 512], BF16)
            rstd_g = M("rstd_g", [128, 512], F32)
            KT2 = M("KT", [128, 2, H, 512], BF16)
            Vg2 = M("Vg", [128, 2, 4, H, 129], BF16)
            PT = M("PT", [128, 4, 4, 128], BF16)
            pt_r = Ring("pt", 4)
            small = M("small", [128, 704], F32)
            scr = M("scr", [128, 6, 512], F32)
            scr_r = Ring("scr", 6)
            acc = M("acc", [128, NR, H, 129], F32)
            rstd_gB = M("rstd_gB", [128, 512], F32)
            rcol = small[:, 0:4]
            fgv = small[:, 8:40]
            e1 = small[:, 40:72]
            spv = small[:, 72:104]
            tot = small[:, 136:168]
            carry = [small[:, 168:176], small[:, 176:184]]
            dlt = small[:, 256:264]
            dltd = small[:, 264:272]
            Sk2 = [small[:, 104:136], small[:, 288:320]]
            sref2 = [small[:, 184:192], small[:, 320:328]]
            bias_g2 = [small[:, 192:224], small[:, 328:360]]
            bias_d2 = [small[:, 224:256], small[:, 360:392]]
            dec2 = [small[:, 272:280], small[:, 392:400]]
            decd2 = [small[:, 280:288], small[:, 400:408]]

            def head_norm(psum_ap, psum_key, T, rstd_ap, rstd_key, gcol, out_ap, out_key, ssq_ps, ssq_key):
                ri, rk = scr_r.next()
                raw = scr[:, ri, :T]
                S.op("dve", lambda e: e.tensor_tensor(out=raw, in0=psum_ap, in1=rstd_ap, op=ALU.mult),
                     reads=[psum_key, rstd_key], writes=[rk])
                qi, qk = sq_r.next()
                S.op("act", lambda e: e.activation(out=sq[:, qi, :T], in_=raw, func=AF.Square), reads=[rk], writes=[qk])
                S.op("pe", lambda e: e.matmul(ssq_ps[:, :T], lhsT=ones_bf[:], rhs=sq[:, qi, :T], start=True, stop=True),
                     reads=[qk, "ones_bf"], writes=[ssq_key])
                r2i, r2k = scr_r.next()
                rn = scr[:, r2i, :T]
                rstd_from(rn, r2k, ssq_ps[:, :T], ssq_key, DH)
                S.op("dve", lambda e: e.scalar_tensor_tensor(out=out_ap, in0=raw, scalar=cc(gcol), in1=rn,
                                                             op0=ALU.mult, op1=ALU.mult),
                     reads=[rk, r2k, "cst"], writes=[out_key])

            S.op("dve", lambda e: e.memset(Vg2[:, :, :, :, 128:129], 1.0), writes=["Vg1"])
            S.op("dve", lambda e: e.memset(carry[0], 0.0), writes=["carry0"])

            for i in range(4):
                load_w(wv[:, :, i * 256:(i + 1) * 256], ("wv", i), w_in, 0, KC, OFF_Q + i * 256, 256)
            for i in range(4):
                load_w(wk[:, :, i * 256:(i + 1) * 256], ("wk", i), w_in, 0, KC, OFF_K + i * 256, 256)
            load_w(wfg[:], "wfg", w_in, 0, KC, OFF_FG, 8)

            NOP = lambda: None
            misc = pb[2]
            xg_d, rstd_g_d = xg, rstd_g
            xgB = KT2[:].rearrange("p a h t -> p (a h) t")
            xgB_key = ("multi", ("KT", 0), ("KT", 1))

            def prep_thunks_src(src_, T, xg=None, xg_key="xg", rstd_g=None, rstd_key="rstd", sb_=2):
                xg = xg_d if xg is None else xg
                rstd_g = rstd_g_d if rstd_g is None else rstd_g
                th = []
                st = {}

                def p1(kc):
                    si, sk = stage_r.next()
                    S.dma("sp", stage[:, si, :T], src_[kc * 128:(kc + 1) * 128, :], writes=[sk])
                    qi, qk = sq_r.next()
                    S.op("dve", lambda e: e.tensor_tensor(out=sq[:, qi, :T], in0=stage[:, si, :T], in1=stage[:, si, :T], op=ALU.mult),
                         reads=[sk], writes=[qk])
                    S.op("dve", lambda e: e.tensor_scalar(out=xg[:, kc, :T], in0=stage[:, si, :T],
                                                          scalar1=cc(C_GMIX + kc), scalar2=None, op0=ALU.mult),
                         reads=[sk, "cst"], writes=[xg_key])
                    st[kc] = (qi, qk)

                def p2(kc):
                    qi, qk = st[kc]
                    S.op("pe", lambda e: e.matmul(pb[sb_][:, :T], lhsT=ones_bf[:], rhs=sq[:, qi, :T],
                                                  start=(kc == 0), stop=(kc == KC - 1)),
                         reads=[qk, "ones_bf"], writes=[pk[sb_]])

                for s in range(KC + 2):
                    def f(s=s):
                        if s < KC:
                            p1(s)
                        if s >= 2:
                            p2(s - 2)
                    th.append(f)
                th.append(NOP)
                th.append(lambda: rstd_from(rstd_g[:, :T], rstd_key, pb[sb_][:, :T], pk[sb_], D))
                return th

            def norm_item(wbuf, wname, h, gcol, out_ap, out_key, T, xg=None, xg_key="xg", rstd_g=None, rstd_key="rstd"):
                xg = xg_d if xg is None else xg
                rstd_g = rstd_g_d if rstd_g is None else rstd_g
                st = {}
                def sA(half):
                    if half == 0:
                        bi, bk = proj_r.next()
                        st["bank"] = (pb[bi], pk[bi])
                    bank, bkey = st["bank"]
                    def mm(e):
                        for kc in range(half * 8, half * 8 + 8):
                            ins = e.matmul(bank[:, :T], lhsT=wbuf[:, kc, h * 128:(h + 1) * 128], rhs=xg[:, kc, :T],
                                           start=(kc == 0), stop=(kc == KC - 1))
                        return ins
                    S.op("pe", mm, reads=[(wname, h // 2), xg_key], writes=[bkey])
                def sB1():
                    bank, bkey = st["bank"]
                    ri, rk = scr_r.next()
                    raw = scr[:, ri, :T]
                    S.op("dve", lambda e: e.tensor_tensor(out=raw, in0=bank[:, :T], in1=rstd_g[:, :T], op=ALU.mult),
                         reads=[bkey, rstd_key], writes=[rk])
                    st["raw"] = (raw, rk)
                def sB2():
                    raw, rk = st["raw"]
                    qi, qk = sq_r.next()
                    S.op("dve", lambda e: e.tensor_tensor(out=sq[:, qi, :T], in0=raw, in1=raw, op=ALU.mult), reads=[rk], writes=[qk])
                    st["sq"] = (qi, qk)
                def sC1():
                    qi, qk = st["sq"]
                    S.op("pe", lambda e: e.matmul(pb[2][:, :T], lhsT=ones_bf[:], rhs=sq[:, qi, :T], start=True, stop=True),
                         reads=[qk, "ones_bf"], writes=[pk[2]])
                def sC1b():
                    r2i, r2k = scr_r.next()
                    rn = scr[:, r2i, :T]
                    S.op("dve", lambda e: e.tensor_copy(out=rn, in_=pb[2][:, :T]), reads=[pk[2]], writes=[r2k])
                    st["rn"] = (rn, r2k)
                def sC2():
                    rn, r2k = st["rn"]
                    rstd_from(rn, r2k, rn, r2k, DH)
                def sD():
                    raw, rk = st["raw"]
                    rn, r2k = st["rn"]
                    S.op("dve", lambda e: e.scalar_tensor_tensor(out=out_ap, in0=raw, scalar=cc(gcol), in1=rn,
                                                                 op0=ALU.mult, op1=ALU.mult),
                         reads=[rk, r2k, "cst"], writes=[out_key])
                return [lambda: sA(0), lambda: sA(1), NOP, sB1, sB2, sC1, NOP, sC1b, sC2, sD]

            def pipeline(items):
                th = []
                nit = len(items)
                for s in range(2 * nit + 10):
                    def f(s=s):
                        for i in range(nit):
                            d = s - 2 * i
                            if 0 <= d < len(items[i]):
                                items[i][d]()
                    th.append(f)
                return th

            for f in prep_thunks_src(xoT[:, 0:512], 512):
                f()
            pa = pipeline([norm_item(wv, "wv", h, C_GQ, QT[:, h, 0:512], ("QT", 0), 512) for h in range(H)])
            pb_ = prep_thunks_src(xoT[:, 512:1024], 512, xgB, xgB_key, rstd_gB, "rstdB", 3)
            for i in range(max(len(pa), len(pb_))):
                if i < len(pa):
                    pa[i]()
                if i < len(pb_):
                    pb_[i]()
            for f in pipeline([norm_item(wv, "wv", h, C_GQ, QT[:, h, 512:1024], ("QT", 1), 512, xgB, xgB_key, rstd_gB, "rstdB")
                               for h in range(H)]):
                f()
            for i in range(4):
                load_w(wv[:, :, i * 256:(i + 1) * 256], ("wv", i), w_in, 0, KC, OFF_V + i * 256, 256)

            groups = [(0, 1, NMETA)] + [(NMETA + 512 * g, 4, 128) for g in range(8)]
            def prep_thunks(g):
                tok0, nb, bt = groups[g]
                return prep_thunks_src(xT[:, tok0:tok0 + nb * bt], nb * bt)

            def prep(g):
                for f in prep_thunks(g):
                    f()

            def proj_thunks(gi):
                tok0, nb, bt = groups[gi]
                T = nb * bt
                par = gi % 2
                KT, Vg = KT2[:, par], Vg2[:, par]
                kKT, kVg = ("KT", par), ("Vg", par)
                Sk, sref, bias_g, bias_d, dec, decd = Sk2[par], sref2[par], bias_g2[par], bias_d2[par], dec2[par], decd2[par]
                kS = lambda n: (n, par)
                c_old, c_new = carry[par], carry[1 - par]
                ko, kn = "carry%d" % par, "carry%d" % (1 - par)

                def k_item(h):
                    return norm_item(wk, "wk", h, C_GK, KT[:, h, :T], kKT, T)

                def v_item(j, ch):
                    st = {}
                    def sA(half):
                        if half == 0:
                            bi, bk = proj_r.next()
                            st["bank"] = (pb[bi], pk[bi])
                        bank, bkey = st["bank"]
                        def mm(e):
                            for kc in range(half * 8, half * 8 + 8):
                                ins = e.matmul(bank[:bt, :], lhsT=xg[:, kc, j * bt:(j + 1) * bt],
                                               rhs=wv[:, kc, ch * 512:(ch + 1) * 512],
                                               start=(kc == 0), stop=(kc == KC - 1))
                            return ins
                        S.op("pe", mm, reads=[("wv", 2 * ch), ("wv", 2 * ch + 1), "xg"], writes=[bkey])
                    def sB():
                        bank, bkey = st["bank"]
                        S.op("dve", lambda e: e.tensor_scalar(
                            out=Vg[:bt, j, ch * 4:(ch + 1) * 4, 0:128],
                            in0=bank[:bt, :].rearrange("p (a b) -> p a b", a=4),
                            scalar1=rcol[:bt, j:j + 1], scalar2=None, op0=ALU.mult),
                            reads=[bkey, "rcol"], writes=[kVg])
                    return [lambda: sA(0), lambda: sA(1), NOP, sB]

                def rcolA():
                    def mm_rcol(e):
                        for j in range(nb):
                            ins = e.matmul(misc[:bt, 256 + j:257 + j], lhsT=rstd_g[:, j * bt:(j + 1) * bt], rhs=id_f[:, 0:1],
                                           start=True, stop=True)
                        return ins
                    S.op("pe", mm_rcol, reads=["rstd", "cmat"], writes=[pk[2]])
                def rcolB():
                    S.op("dve", lambda e: e.tensor_copy(out=rcol[:bt, 0:nb], in_=misc[:bt, 256:256 + nb]),
                         reads=[pk[2]], writes=["rcol"])

                def g1():
                    def mm_fg(e):
                        for j in range(nb):
                            for kc in range(KC):
                                ins = e.matmul(misc[:bt, j * 8:(j + 1) * 8], lhsT=xg[:, kc, j * bt:(j + 1) * bt],
                                               rhs=wfg[:, kc, :], start=(kc == 0), stop=(kc == KC - 1))
                        return ins
                    S.op("pe", mm_fg, reads=["wfg", "xg"], writes=[pk[2]])
                def g2():
                    for j in range(nb):
                        S.op("dve", lambda e, j=j: e.scalar_tensor_tensor(
                            out=fgv[:bt, j * 8:(j + 1) * 8], in0=misc[:bt, j * 8:(j + 1) * 8], scalar=rcol[:bt, j:j + 1],
                            in1=cst[:bt, C_BFG:C_BFG + 8], op0=ALU.mult, op1=ALU.add),
                            reads=[pk[2], "rcol", "cst"], writes=["fgv"])
                def g3():
                    S.op("act", lambda e: e.activation(out=e1[:bt, :nb * 8], in_=fgv[:bt, :nb * 8], func=AF.Exp, scale=-1.0),
                         reads=["fgv"], writes=["e1"])
                    S.op("act", lambda e: e.activation(out=spv[:bt, :nb * 8], in_=e1[:bt, :nb * 8], func=AF.Ln, bias=cst[:bt, C_ONE:C_ONE + 1]),
                         reads=["e1", "cst"], writes=["spv"])
                def g4():
                    def mm_cum(e):
                        for j in range(nb):
                            ins = e.matmul(misc[:bt, 64 + j * 8:64 + (j + 1) * 8], lhsT=U_f[:bt, :bt], rhs=spv[:bt, j * 8:(j + 1) * 8],
                                           start=True, stop=(j == 0))
                            for i in range(j):
                                ins = e.matmul(misc[:bt, 64 + j * 8:64 + (j + 1) * 8], lhsT=ones_f[:bt, :bt], rhs=spv[:bt, i * 8:(i + 1) * 8],
                                               start=False, stop=(i == j - 1))
                        for j in range(nb):
                            ins = e.matmul(misc[:, 128 + j * 8:128 + (j + 1) * 8], lhsT=ones_f[:bt, :], rhs=spv[:bt, j * 8:(j + 1) * 8],
                                           start=True, stop=True)
                        return ins
                    S.op("pe", mm_cum, reads=["spv", "cmat"], writes=[pk[2]])
                def g5():
                    S.op("dve", lambda e: e.tensor_copy(out=tot[:, :nb * 8], in_=misc[:, 128:128 + nb * 8]),
                         reads=[pk[2]], writes=["tot"])
                    for j in range(nb):
                        S.op("dve", lambda e, j=j: e.tensor_tensor(
                            out=Sk[:bt, j * 8:(j + 1) * 8], in0=misc[:bt, 64 + j * 8:64 + (j + 1) * 8], in1=c_old[:bt], op=ALU.add),
                            reads=[pk[2], ko], writes=[kS("Sk")])
                    for j in range(nb):
                        src_ = c_old if j == 0 else c_new
                        S.op("dve", lambda e, j=j, src_=src_: e.tensor_tensor(out=c_new, in0=src_, in1=tot[:, j * 8:(j + 1) * 8], op=ALU.add),
                             reads=["tot", ko, kn], writes=[kn])
                    S.op("dve", lambda e: e.tensor_tensor(out=dlt, in0=c_new, in1=c_old, op=ALU.subtract),
                         reads=[ko, kn], writes=["dlt"])
                    for j in range(nb):
                        S.op("dve", lambda e, j=j: e.tensor_tensor(
                            out=bias_g[:bt, j * 8:(j + 1) * 8], in0=Sk[:bt, j * 8:(j + 1) * 8], in1=c_new[:bt], op=ALU.subtract),
                            reads=[kS("Sk"), kn], writes=[kS("bias_g")])
                    if gi >= 1:
                        for j in range(nb):
                            src_ = c_old if j == 0 else sref
                            S.op("dve", lambda e, j=j, src_=src_: e.scalar_tensor_tensor(
                                out=sref, in0=tot[:, j * 8:(j + 1) * 8], scalar=cc(C_PRESEL + j), in1=src_, op0=ALU.mult, op1=ALU.add),
                                reads=["tot", ko, kS("sref"), "cst"], writes=[kS("sref")])
                        S.op("dve", lambda e: e.tensor_tensor(out=dltd, in0=sref, in1=c_old, op=ALU.subtract),
                             reads=[kS("sref"), ko], writes=["dltd"])
                        for j in range(nb):
                            S.op("dve", lambda e, j=j: e.scalar_tensor_tensor(
                                out=bias_d[:, j * 8:(j + 1) * 8], in0=Sk[:, j * 8:(j + 1) * 8], scalar=cc(C_MASKB + j), in1=sref,
                                op0=ALU.add, op1=ALU.subtract), reads=[kS("Sk"), kS("sref"), "cst"], writes=[kS("bias_d")])
                def g6():
                    S.op("act", lambda e: e.activation(out=dec, in_=dlt, func=AF.Exp, scale=-1.0), reads=["dlt"], writes=[kS("dec")])
                    if gi >= 1:
                        S.op("act", lambda e: e.activation(out=decd, in_=dltd, func=AF.Exp, scale=-1.0), reads=["dltd"], writes=[kS("decd")])

                items = [[rcolA, NOP, NOP, rcolB]] + [k_item(h) for h in range(H)] + [v_item(j, ch) for j in range(nb) for ch in range(2)]
                th = pipeline(items)
                th += [g1, NOP, g2, g3, NOP, g4, NOP, g5, g6]
                return th

            def attn_units(gi):
                tok0, nb, bt = groups[gi]
                par = gi % 2
                KT, Vg = KT2[:, par], Vg2[:, par]
                kKT, kVg = ("KT", par), ("Vg", par)
                rounds = range(NR) if gi == 0 else range(gi - 1, NR)
                out = []
                for r in rounds:
                    for h in range(H):
                        st = {}
                        diag = gi >= 1 and r == gi - 1
                        bias_t, bias_k = (bias_d2[par], ("bias_d", par)) if diag else (bias_g2[par], ("bias_g", par))
                        dec_t, dec_k = (decd2[par], ("decd", par)) if diag else (dec2[par], ("dec", par))

                        def emit_s(r=r, h=h, st=st):
                            si, _ = s_r.next()
                            sbank, skey = pb[3 + si], pk[3 + si]
                            def mm_s(e):
                                for j in range(nb):
                                    ins = e.matmul(sbank[:bt, j * 128:(j + 1) * 128], lhsT=KT[:, h, j * bt:(j + 1) * bt],
                                                   rhs=QT[:, h, r * 128:(r + 1) * 128], start=True, stop=True)
                                return ins
                            S.op("pe", mm_s, reads=[kKT, ("QT", 0), ("QT", 1)], writes=[skey])
                            st["s"] = (sbank, skey)

                        def er1(r=r, h=h, st=st, diag=diag, bias_t=bias_t, bias_k=bias_k):
                            sbank, skey = st["s"]
                            pi, pkk = pt_r.next()
                            st["p"] = (pi, pkk)
                            for j in range(nb):
                                S.op("act", lambda e, j=j: e.activation(
                                    out=PT[:bt, pi, j, :], in_=sbank[:bt, j * 128:(j + 1) * 128], func=AF.Exp,
                                    bias=bias_t[:bt, j * 8 + h:j * 8 + h + 1], scale=SCALE),
                                    reads=[skey, bias_k], writes=[pkk])
                            if diag:
                                S.op("dve", lambda e: e.tensor_tensor(out=PT[:, pi], in0=PT[:, pi], in1=tri_bf[:], op=ALU.mult),
                                     reads=[pkk, "tri_bf"], writes=[pkk])

                        def er2(r=r, h=h, st=st):
                            pi, pkk = st["p"]
                            oi, _ = o_r.next()
                            okey = pk[6 + oi]
                            otile = pb[6 + oi][:, 0:129]
                            st["o"] = (otile, okey)
                            def mm_o(e):
                                for j in range(nb):
                                    ins = e.matmul(otile, lhsT=PT[:bt, pi, j, :], rhs=Vg[:bt, j, h, :],
                                                   start=(j == 0), stop=(j == nb - 1))
                                return ins
                            S.op("pe", mm_o, reads=[pkk, kVg, "Vg1"], writes=[okey])

                        def er3(r=r, h=h, st=st, dec_t=dec_t, dec_k=dec_k):
                            otile, okey = st["o"]
                            akey = ("acc", r, h)
                            if gi == 0:
                                S.op("dve", lambda e: e.tensor_copy(out=acc[:, r, h, :], in_=otile),
                                     reads=[okey], writes=[akey])
                            else:
                                S.op("dve", lambda e: e.scalar_tensor_tensor(
                                    out=acc[:, r, h, :], in0=acc[:, r, h, :], scalar=dec_t[:, h:h + 1], in1=otile,
                                    op0=ALU.mult, op1=ALU.add), reads=[okey, akey, dec_k], writes=[akey])
                        out.append((emit_s, er1, er2, er3))
                return out

            prep(0)
            for f in proj_thunks(0):
                f()
            NG = len(groups)
            all_units, sched = [], []
            thunks = {}
            for gi in range(NG):
                units = attn_units(gi)
                pth = proj_thunks(gi + 1) if gi + 1 < NG else []
                if gi == 0:
                    pth = prep_thunks(1) + pth
                if gi + 2 < NG:
                    pth = pth + prep_thunks(gi + 2)
                thunks[gi] = [pth, 0]
                for li in range(len(units)):
                    sched.append((gi, li, len(units)))
                all_units += units
            NU = len(all_units)
            OFFS = (0, 2, 3, 4)
            for i in range(NU + OFFS[-1]):
                for k in range(4):
                    u = i - OFFS[k]
                    if 0 <= u < NU:
                        all_units[u][k]()
                if i < NU:
                    gi, li, nu = sched[i]
                    pth, pj = thunks[gi]
                    npj = len(pth)
                    while pj < npj and ((pj + 1) * nu <= (li + 1) * npj or li == nu - 1):
                        pth[pj]()
                        pj += 1
                    thunks[gi][1] = pj

        S.barrier()
        H0 = P0 + 32768
        Z0 = P0 + 147456
        xog = Bump(nc, P0)("xog", [128, KC, TOWN], BF16)
        h1 = Bump(nc, H0)("h1", [128, KC, TOWN], F32)
        HB = Bump(nc, H0)
        aT = HB("aT", [128, H, TOWN], BF16)
        cT = HB("cT", [128, 8, TOWN], BF16)
        ubuf = HB("ubuf", [128, NR, 130], F32)
        ybuf = HB("ybuf", [128, NR, 128], F32)
        xhg = HB("xhg", [128, KC, 16], BF16)
        rstd_h = HB("rstd_h", [128, 16], F32)
        rstd2_h = HB("rstd2_h", [128, 16], F32)
        atok = HB("atok", [128, 8, 128], BF16)
        recs = HB("recs", [128, NR * H], F32)
        tmpA = HB("tmpA", [128, 4, 512], F32)
        assert HB.off <= H0 + 65536
        CB_ = Bump(nc, P0 + 98304)
        NWS = 10
        wring = CB_("wring", [128, NWS * 2048], BF16)
        rstd_o = CB_("rstd_o", [128, TOWN], F32)
        rstd2_o = CB_("rstd2_o", [128, TOWN], F32)
        assert CB_.off <= Z0
        ZB = Bump(nc, Z0)
        mT = ZB("mT", [128, KC, TOWN], BF16)
        tmp = ZB("tmp", [128, 4, 512], F32)
        uT = mT
        hng = xog
        atok_r = Ring("atok", 8)
        w_r = Ring("w", NWS)
        tmp_r = Ring("tmp", 4)
        with ExitStack() as esc:
            pc = [ps("pc%d" % i, [128, 512], F32, esc) for i in range(8)]
            pck = [("pc", i) for i in range(8)]
            TS = [slice(0, 512), slice(512, 1024)]

            def wtile(dram, row0, kc, col0, ncol):
                nsl = (kc * ncol * 2 + 4095) // 4096
                if w_r.i + nsl > NWS:
                    w_r.i = 0
                i0 = w_r.i
                keys = [w_r.next()[1] for _ in range(nsl)]
                view = wring[:, i0 * 2048:i0 * 2048 + kc * ncol].rearrange("p (k n) -> p k n", k=kc)
                src_ = dram[row0:row0 + kc * 128, col0:col0 + ncol].rearrange("(kc p) n -> p kc n", p=128)
                S.dma("pool", view, src_, writes=keys)
                return view, ("multi",) + tuple(keys)

            for t in range(2):
                rms_prep(xoT[:, TS[t]], 512, xog[:, :, TS[t]], ("xog", t), C_GMIX, rstd_o[:, TS[t]], ("rstd_o", t), pc[7], pck[7])
                S.op("dve", lambda e, t=t: e.tensor_tensor(out=rstd2_o[:, TS[t]], in0=rstd_o[:, TS[t]], in1=rstd_o[:, TS[t]], op=ALU.mult),
                     reads=[("rstd_o", t)], writes=[("rstd2_o", t)])
            rms_prep(xhT[:, :], 16, xhg, "xhg", C_GMIX, rstd_h, "rstd_h", pc[7], pck[7])
            S.op("dve", lambda e: e.tensor_tensor(out=rstd2_h[:], in0=rstd_h[:], in1=rstd_h[:], op=ALU.mult),
                 reads=["rstd_h"], writes=["rstd2_h"])

            tmp_mlp, tmp = tmp, tmpA
            S.op("dve", lambda e: e.reciprocal(out=recs[:].rearrange("p (r h) -> p r h", r=NR), in_=acc[:, :, :, 128]),
                 writes=["recs"])
            c0_chunks = [(h, rh) for h in range(H) for rh in range(2)]
            pc7_bf = pc[7][:, :].bitcast(BF16)
            c0_state = {}

            def c0_a(ci):
                h, rh = c0_chunks[ci]
                slots = []
                for r4 in range(4):
                    r = rh * 4 + r4
                    ai, ak = atok_r.next()
                    S.op("dve", lambda e, r=r, ai=ai: e.tensor_scalar(
                        out=atok[:, ai, :], in0=acc[:, r, h, 0:128], scalar1=recs[:, r * H + h:r * H + h + 1],
                        scalar2=None, op0=ALU.mult), reads=["recs"], writes=[ak])
                    slots.append((ai, ak))
                c0_state[ci] = slots

            def c0_b(ci):
                for r4, (ai, ak) in enumerate(c0_state[ci]):
                    S.op("pe", lambda e, ai=ai, r4=r4: e.transpose(pc7_bf[:, r4 * 128:(r4 + 1) * 128], atok[:, ai, :], id_bf[:]),
                         reads=[ak, "id_bf"], writes=[pck[7]])

            def c0_c(ci):
                h, rh = c0_chunks[ci]
                S.op("act", lambda e: e.copy(out=aT[:, h, rh * 512:(rh + 1) * 512], in_=pc7_bf[:, 0:512]),
                     reads=[pck[7]], writes=[("aT", rh)])

            def c0_step(i):
                if 0 <= i - 2 < 16:
                    c0_c(i - 2)
                if 0 <= i - 1 < 16:
                    c0_b(i - 1)
                if 0 <= i < 16:
                    c0_a(i)

            unit = 0
            for j in range(8):
                wcb, kcb = wtile(w_in, 0, KC, OFF_CB + j * 128, 128)
                wcc, kcc = wtile(w_in, 0, KC, OFF_CC + j * 128, 128)
                wcx, kcx = wtile(w_in, 0, KC, OFF_CX + j * 128, 128)
                for jj in range(1):
                    cs = slice(0, 128)
                    hb = 6
                    def mm_h(e, cs=cs, wcc=wcc, wcx=wcx, hb=hb):
                        for kc in range(KC):
                            ins = e.matmul(pc[hb][:, 0:16], lhsT=wcc[:, kc, cs], rhs=xhg[:, kc, :], start=(kc == 0), stop=(kc == KC - 1))
                        for kc in range(KC):
                            ins = e.matmul(pc[hb][:, 16:32], lhsT=wcx[:, kc, cs], rhs=xhg[:, kc, :], start=(kc == 0), stop=(kc == KC - 1))
                        return ins
                    S.op("pe", mm_h, reads=[kcc, kcx, "xhg"], writes=[pck[hb]])
                    ti, tk = tmp_r.next()
                    S.op("dve", lambda e, ti=ti, hb=hb: e.tensor_tensor(out=tmp[:, ti, 0:16], in0=pc[hb][:, 0:16], in1=rstd2_h[:], op=ALU.mult),
                         reads=[pck[hb], "rstd2_h"], writes=[tk])
                    S.op("dve", lambda e, ti=ti, hb=hb: e.tensor_tensor(out=ubuf[:, :, 0:2], in0=tmp[:, ti, 0:16].rearrange("p (r c) -> p r c", c=2),
                                                                        in1=pc[hb][:, 16:32].rearrange("p (r c) -> p r c", c=2), op=ALU.mult),
                         reads=[tk, pck[hb]], writes=[("ubuf", 0), ("ubuf", 1)])
                    for t in range(2):
                        b0 = 3 * (unit % 2)
                        unit += 1
                        rs = slice(t * 4, (t + 1) * 4)
                        def mm3(e, cs=cs, wcb=wcb, wcc=wcc, wcx=wcx, b0=b0, t=t):
                            for bank, w in ((pc[b0], wcc), (pc[b0 + 1], wcx), (pc[b0 + 2], wcb)):
                                for kc in range(KC):
                                    ins = e.matmul(bank[:, :], lhsT=w[:, kc, cs], rhs=xog[:, kc, TS[t]], start=(kc == 0), stop=(kc == KC - 1))
                            return ins
                        S.op("pe", mm3, reads=[kcb, kcc, kcx, ("xog", t)], writes=[pck[b0], pck[b0 + 1], pck[b0 + 2]])
                        c0_step(unit - 1)
                        ti, tk = tmp_r.next()
                        S.op("dve", lambda e, ti=ti, b0=b0, t=t: e.tensor_tensor(out=tmp[:, ti, :], in0=pc[b0][:, :], in1=rstd2_o[:, TS[t]], op=ALU.mult),
                             reads=[pck[b0], ("rstd2_o", t)], writes=[tk])
                        S.op("dve", lambda e, ti=ti, b0=b0, rs=rs: e.tensor_tensor(
                            out=ubuf[:, rs, 2:130], in0=tmp[:, ti, :].rearrange("p (r c) -> p r c", r=4),
                            in1=pc[b0 + 1][:, :].rearrange("p (r c) -> p r c", r=4), op=ALU.mult),
                            reads=[tk, pck[b0 + 1]], writes=[("ubuf", t)])
                        cw = lambda tap, j=j: cc(C_CONV + j * 3 + tap)
                        yk = ("ybuf", t)
                        S.op("dve", lambda e, cw=cw, rs=rs: e.tensor_scalar(out=ybuf[:, rs, :], in0=ubuf[:, rs, 0:128], scalar1=cw(0), scalar2=None, op0=ALU.mult),
                             reads=[("ubuf", t), "cst"], writes=[yk])
                        S.op("dve", lambda e, cw=cw, rs=rs: e.scalar_tensor_tensor(out=ybuf[:, rs, :], in0=ubuf[:, rs, 1:129], scalar=cw(1), in1=ybuf[:, rs, :],
                                                                                   op0=ALU.mult, op1=ALU.add), reads=[("ubuf", t), "cst", yk], writes=[yk])
                        S.op("dve", lambda e, cw=cw, rs=rs: e.scalar_tensor_tensor(out=ybuf[:, rs, :], in0=ubuf[:, rs, 2:130], scalar=cw(2), in1=ybuf[:, rs, :],
                                                                                   op0=ALU.mult, op1=ALU.add), reads=[("ubuf", t), "cst", yk], writes=[yk])
                        ti, tk = tmp_r.next()
                        S.op("dve", lambda e, ti=ti, b0=b0, t=t: e.tensor_tensor(out=tmp[:, ti, :], in0=pc[b0 + 2][:, :], in1=rstd_o[:, TS[t]], op=ALU.mult),
                             reads=[pck[b0 + 2], ("rstd_o", t)], writes=[tk])
                        S.op("dve", lambda e, ti=ti, j=j, t=t, rs=rs: e.tensor_tensor(
                            out=cT[:, j, TS[t]], in0=tmp[:, ti, :], in1=ybuf[:, rs, :].rearrange("p r c -> p (r c)"), op=ALU.mult),
                            reads=[tk, yk], writes=[("cT", t)])

            c0_step(16)
            c0_step(17)

            unit = 0
            for cb in range(16):
                wa, ka = wtile(w_ao, 0, 8, cb * 128, 128)
                wc_, kc_ = wtile(w_co, 0, 8, cb * 128, 128)
                wg0, kg0 = wtile(w_in, 0, KC, OFF_GL + cb * 128, 128)
                wg1, kg1 = wtile(w_in, 0, KC, OFF_GL + D + cb * 128, 128)
                for jj in range(1):
                    cs = slice(0, 128)
                    for t in range(2):
                        b0 = 4 * (unit % 2)
                        unit += 1
                        def mm4(e, cs=cs, t=t, wa=wa, wc_=wc_, wg0=wg0, wg1=wg1, b0=b0):
                            for hh in range(8):
                                ins = e.matmul(pc[b0][:, :], lhsT=wa[:, hh, cs], rhs=aT[:, hh, TS[t]], start=(hh == 0), stop=(hh == 7))
                            for hh in range(8):
                                ins = e.matmul(pc[b0 + 1][:, :], lhsT=wc_[:, hh, cs], rhs=cT[:, hh, TS[t]], start=(hh == 0), stop=(hh == 7))
                            for kc in range(KC):
                                ins = e.matmul(pc[b0 + 2][:, :], lhsT=wg0[:, kc, cs], rhs=xog[:, kc, TS[t]], start=(kc == 0), stop=(kc == KC - 1))
                            for kc in range(KC):
                                ins = e.matmul(pc[b0 + 3][:, :], lhsT=wg1[:, kc, cs], rhs=xog[:, kc, TS[t]], start=(kc == 0), stop=(kc == KC - 1))
                            return ins
                        S.op("pe", mm4, reads=[ka, kc_, kg0, kg1, ("aT", t), ("cT", t), ("xog", t)],
                             writes=[pck[b0], pck[b0 + 1], pck[b0 + 2], pck[b0 + 3]])
                        t0i, t0k = tmp_r.next()
                        t1i, t1k = tmp_r.next()
                        for (ti, tk, bank, boff) in ((t0i, t0k, b0 + 2, 0), (t1i, t1k, b0 + 3, 16)):
                            S.op("dve", lambda e, ti=ti, bank=bank, t=t: e.tensor_tensor(out=tmp[:, ti, :], in0=pc[bank][:, :], in1=rstd_o[:, TS[t]], op=ALU.mult),
                                 reads=[pck[bank], ("rstd_o", t)], writes=[tk])
                            S.op("act", lambda e, ti=ti, cb=cb, boff=boff: e.activation(out=tmp[:, ti, :], in_=tmp[:, ti, :], func=AF.Sigmoid,
                                                                                    bias=cc(C_BGATE + boff + cb)), reads=[tk, "cst"], writes=[tk])
                        S.op("dve", lambda e, t0i=t0i, b0=b0: e.tensor_tensor(out=tmp[:, t0i, :], in0=tmp[:, t0i, :], in1=pc[b0][:, :], op=ALU.mult),
                             reads=[t0k, pck[b0]], writes=[t0k])
                        S.op("dve", lambda e, t1i=t1i, b0=b0: e.tensor_tensor(out=tmp[:, t1i, :], in0=tmp[:, t1i, :], in1=pc[b0 + 1][:, :], op=ALU.mult),
                             reads=[t1k, pck[b0 + 1]], writes=[t1k])
                        S.op("dve", lambda e, t0i=t0i, t1i=t1i, cb=cb, t=t: e.tensor_tensor(out=mT[:, cb, TS[t]], in0=tmp[:, t0i, :], in1=tmp[:, t1i, :], op=ALU.add),
                             reads=[t0k, t1k], writes=[("mT", t)])

            unit = 0
            pend = []
            for c2 in range(8):
                wo, ko_ = wtile(w_o, 0, KC, c2 * 256, 256)
                for jj in range(2):
                    cb = c2 * 2 + jj
                    cs = slice(jj * 128, (jj + 1) * 128)
                    for t in range(2):
                        bank = unit % 4
                        unit += 1
                        def mm(e, cs=cs, wo=wo, bank=bank, t=t):
                            for kc in range(KC):
                                ins = e.matmul(pc[bank][:, :], lhsT=wo[:, kc, cs], rhs=mT[:, kc, TS[t]], start=(kc == 0), stop=(kc == KC - 1))
                            return ins
                        S.op("pe", mm, reads=[ko_, ("mT", t)], writes=[pck[bank]])
                        si, sk = stage_r.next()
                        S.dma("sp", stage[:, si, :], xoT[cb * 128:(cb + 1) * 128, TS[t]], writes=[sk])
                        hk = ("h1", cb, t)
                        S.op("dve", lambda e, si=si, cb=cb, bank=bank, t=t: e.tensor_tensor(out=h1[:, cb, TS[t]], in0=pc[bank][:, :], in1=stage[:, si, :], op=ALU.add),
                             reads=[pck[bank], sk], writes=[hk])
                        qi, qk = sq_r.next()
                        S.op("act", lambda e, qi=qi, cb=cb, t=t: e.activation(out=sq[:, qi, :], in_=h1[:, cb, TS[t]], func=AF.Square),
                             reads=[hk], writes=[qk])
                        pend.append(lambda qi=qi, qk=qk, cb=cb, t=t: S.op(
                            "pe", lambda e: e.matmul(pc[6 + t][:, :], lhsT=ones_bf[:], rhs=sq[:, qi, :], start=(cb == 0), stop=(cb == KC - 1)),
                            reads=[qk, "ones_bf"], writes=[pck[6 + t]]))
                        if len(pend) > 2:
                            pend.pop(0)()
                        S.op("dve", lambda e, cb=cb, t=t: e.tensor_scalar(out=hng[:, cb, TS[t]], in0=h1[:, cb, TS[t]], scalar1=cc(C_GMLP + cb),
                                                                          scalar2=None, op0=ALU.mult), reads=[hk, "cst"], writes=[("xog", t)])
            while pend:
                pend.pop(0)()
            for t in range(2):
                rstd_from(rstd_o[:, TS[t]], ("rstd_o", t), pc[6 + t][:, :], pck[6 + t], D)

            tmp = tmp_mlp
            NQ = 4
            FQ = DFF // NQ
            unit = 0
            for fq in range(NQ):
                for f2 in range(8):
                    wu, ku = wtile(w_up, 0, KC, fq * FQ + f2 * 256, 256)
                    for jj in range(2):
                        fb = f2 * 2 + jj
                        cs = slice(jj * 128, (jj + 1) * 128)
                        for t in range(2):
                            bank = unit % 4
                            unit += 1
                            def mm(e, cs=cs, wu=wu, bank=bank, t=t):
                                for kc in range(KC):
                                    ins = e.matmul(pc[bank][:, :], lhsT=wu[:, kc, cs], rhs=hng[:, kc, TS[t]], start=(kc == 0), stop=(kc == KC - 1))
                                return ins
                            S.op("pe", mm, reads=[ku, ("xog", t)], writes=[pck[bank]])
                            ti, tk = tmp_r.next()
                            S.op("dve", lambda e, ti=ti, bank=bank, t=t: e.scalar_tensor_tensor(
                                out=tmp[:, ti, :], in0=pc[bank][:, :], scalar=0.0, in1=rstd_o[:, TS[t]], op0=ALU.max, op1=ALU.mult),
                                reads=[pck[bank], ("rstd_o", t)], writes=[tk])
                            S.op("act", lambda e, ti=ti, fb=fb, t=t: e.activation(out=uT[:, fb, TS[t]], in_=tmp[:, ti, :], func=AF.Square),
                                 reads=[tk], writes=[("mT", t)])
                for c2 in range(8):
                    wd, kd = wtile(w_down, fq * FQ, KC, c2 * 256, 256)
                    for jj in range(2):
                        cb = c2 * 2 + jj
                        cs = slice(jj * 128, (jj + 1) * 128)
                        for t in range(2):
                            bank = 4 + unit % 4
                            unit += 1
                            def mm(e, wd=wd, bank=bank, cs=cs, t=t):
                                for kc in range(KC):
                                    ins = e.matmul(pc[bank][:, :], lhsT=wd[:, kc, cs], rhs=uT[:, kc, TS[t]], start=(kc == 0), stop=(kc == KC - 1))
                                return ins
                            S.op("pe", mm, reads=[kd, ("mT", t)], writes=[pck[bank]])
                            hk = ("h1", cb, t)
                            S.op("dve", lambda e, cb=cb, bank=bank, t=t: e.tensor_tensor(out=h1[:, cb, TS[t]], in0=h1[:, cb, TS[t]], in1=pc[bank][:, :], op=ALU.add),
                                 reads=[pck[bank], hk], writes=[hk])
                            if fq == NQ - 1:
                                S.dma("sp", yT[cb * 128:(cb + 1) * 128, TS[t]], h1[:, cb, TS[t]], reads=[hk], writes=[("yT", cb, t)])
            S.drain("sp")
            S.drain("pool")
    return nc


_PROGRAM = None


def _consts(p, inp):
    cst = np.zeros((128, NCST), np.float32)
    cst[:, C_GMIX:C_GMIX + 16] = inp["norm_mix"][0].reshape(16, 128).T
    cst[:, C_GMLP:C_GMLP + 16] = inp["norm_mlp"][0].reshape(16, 128).T
    cst[:, C_GQ] = inp["q_norm"][0]
    cst[:, C_GK] = inp["k_norm"][0]
    cst[:, C_BFG:C_BFG + 8] = inp["b_fgate"][0][None, :]
    cst[:, C_BGATE:C_BGATE + 32] = inp["b_gate"][0].reshape(32, 128).T
    cw = inp["conv_w"][0]
    for j in range(8):
        for tap in range(3):
            cst[:, C_CONV + j * 3 + tap] = cw[tap, j * 128:(j + 1) * 128]
    for s in range(4):
        cst[:, C_MASKB + s] = 0.0 if s <= p else NEG
        cst[:, C_PRESEL + s] = 1.0 if s < p else 0.0
    cst[:, C_EPS] = EPS
    cst[:, C_ONE] = 1.0
    cmat = np.zeros((128, NMAT), np.float32)
    k = np.arange(128)[:, None]
    t = np.arange(128)[None, :]
    cmat[:, M_ONES:M_ONES + 128] = 1.0
    cmat[:, M_U:M_U + 128] = (k <= t)
    cmat[:, M_ID:M_ID + 128] = (k == t)
    for s in range(4):
        if s < p:
            m = np.ones((128, 128), np.float32)
        elif s == p:
            m = (k <= t).astype(np.float32)
        else:
            m = np.zeros((128, 128), np.float32)
        cmat[:, M_TRI + s * 128:M_TRI + (s + 1) * 128] = m
    return cst, cmat


def kernel(x, meta_tokens, norm_mix, w_in, b_fgate, b_gate, q_norm, k_norm, conv_w,
           w_attn_out, w_conv_out, w_o, norm_mlp, w_up, w_down):
    global _PROGRAM
    inp = dict(norm_mix=norm_mix, norm_mlp=norm_mlp, q_norm=q_norm, k_norm=k_norm, b_fgate=b_fgate,
               b_gate=b_gate, conv_w=conv_w)
    x = np.asarray(x, np.float32)
    B = x.shape[0]
    shared = dict(
        w_in=np.ascontiguousarray(w_in[0], np.float32), w_ao=np.ascontiguousarray(w_attn_out[0], np.float32),
        w_co=np.ascontiguousarray(w_conv_out[0], np.float32), w_o=np.ascontiguousarray(w_o[0], np.float32),
        w_up=np.ascontiguousarray(w_up[0], np.float32), w_down=np.ascontiguousarray(w_down[0], np.float32))
    in_maps = []
    fullT = []
    for b in range(B):
        full = np.concatenate([np.asarray(meta_tokens, np.float32), x[b]], axis=0)
        fullT.append((full, np.ascontiguousarray(full.T)))
    for c in range(8):
        b, p = c // 4, c % 4
        full, fT = fullT[b]
        own_idx = np.concatenate([np.arange(NMETA + 128 * (4 * r + p), NMETA + 128 * (4 * r + p + 1)) for r in range(NR)])
        halo_idx = np.concatenate([np.array([NMETA + 128 * (4 * r + p) - 2, NMETA + 128 * (4 * r + p) - 1]) for r in range(NR)])
        cst, cmat = _consts(p, inp)
        m = dict(shared)
        m.update(xT=fT, xoT=np.ascontiguousarray(full[own_idx].T), xhT=np.ascontiguousarray(full[halo_idx].T),
                 cst=cst, cmat=cmat)
        in_maps.append(m)
    if _PROGRAM is None:
        _PROGRAM = build_program()
    res = run_bass_kernel_spmd(_PROGRAM, in_maps, core_ids=list(range(8)))
    out = np.empty((B, SEQ, D), np.float32)
    for c in range(8):
        b, p = c // 4, c % 4
        y = res.results[c]["yT"]
        for r in range(NR):
            s0 = 128 * (4 * r + p)
            out[b, s0:s0 + 128, :] = y[:, r * 128:(r + 1) * 128].T
    return out
```

```python
import numpy as np
from contextlib import ExitStack
import concourse.bass as bass
import concourse.mybir as mybir
from concourse.bass_utils import run_bass_kernel_spmd

F32, BF16 = mybir.dt.float32, mybir.dt.bfloat16
AF = mybir.ActivationFunctionType
ALU = mybir.AluOpType

D = 2048
KC = 16
NMETA = 16
SEQ = 4096
L = SEQ + NMETA
H = 8
DH = 128
NR = 8
TOWN = NR * 128
IN_COLS = 10248
OFF_Q, OFF_K, OFF_V, OFF_FG = 0, 1024, 2048, 3072
OFF_CB, OFF_CC, OFF_CX, OFF_GL = 3080, 4104, 5128, 6152
DFF = 8192
EPS = 1e-6
SCALE = DH ** -0.5
NEG = -30000.0

C_GMIX, C_GMLP, C_GQ, C_GK, C_BFG, C_BGATE, C_CONV, C_MASKB, C_PRESEL = 0, 16, 32, 33, 34, 42, 74, 98, 102
C_EPS, C_ONE = 106, 107
NCST = 108
M_ONES, M_U, M_ID, M_TRI = 0, 128, 256, 384
NMAT = 896


class Sched:
    NDSEM = 8

    def __init__(self, nc, es):
        self.nc = nc
        self.eng = {"pe": nc.tensor, "act": nc.scalar, "dve": nc.vector, "pool": nc.gpsimd, "sp": nc.sync}
        self.sem = {e: es.enter_context(nc.semaphore("sem_" + e)) for e in self.eng}
        self.cnt = {e: 0 for e in self.eng}
        self.waited = {e: {} for e in self.eng}
        self.lastw = {}
        self.readers = {}
        self.dsem = {q: [es.enter_context(nc.semaphore("d_%s_%d" % (q, i))) for i in range(self.NDSEM)]
                     for q in ("sp", "pool")}
        self.duse = {q: [0] * self.NDSEM for q in ("sp", "pool")}
        self.dnext = {q: 0 for q in ("sp", "pool")}

    def _wait(self, e, tok):
        key, sem, val = tok
        if e == "pe" and key == "pe":
            return
        if self.waited[e].get(key, 0) >= val:
            return
        self.eng[e].wait_ge(sem, val)
        self.waited[e][key] = val

    @staticmethod
    def _expand(keys):
        out = []
        for k in keys:
            if isinstance(k, tuple) and len(k) > 0 and k[0] == "multi":
                out.extend(k[1:])
            else:
                out.append(k)
        return out

    def _deps(self, reads, writes):
        reads, writes = self._expand(reads), self._expand(writes)
        best = {}
        def add(t):
            if t[0] not in best or best[t[0]][2] < t[2]:
                best[t[0]] = t
        for k in reads:
            if k in self.lastw:
                add(self.lastw[k])
        for k in writes:
            if k in self.lastw:
                add(self.lastw[k])
            for t in self.readers.get(k, {}).values():
                add(t)
        return list(best.values())

    def _record(self, tok, reads, writes):
        reads, writes = self._expand(reads), self._expand(writes)
        for k in writes:
            self.lastw[k] = tok
            self.readers[k] = {}
        for k in reads:
            self.readers.setdefault(k, {})[tok[0]] = tok

    def op(self, e, fn, reads=(), writes=()):
        for t in self._deps(reads, writes):
            self._wait(e, t)
        ins = fn(self.eng[e])
        self.cnt[e] += 1
        ins.then_inc(self.sem[e], 1)
        tok = (e, self.sem[e], self.cnt[e])
        self._record(tok, reads, writes)
        return tok

    def dma(self, q, out, in_, reads=(), writes=()):
        for t in self._deps(reads, writes):
            self._wait(q, t)
        i = self.dnext[q]
        self.dnext[q] = (i + 1) % self.NDSEM
        sem = self.dsem[q][i]
        key = "d_%s_%d" % (q, i)
        if self.duse[q][i] > 0:
            self._wait(q, (key, sem, 16 * self.duse[q][i]))
        self.eng[q].dma_start(out=out, in_=in_).then_inc(sem, 16)
        self.duse[q][i] += 1
        tok = (key, sem, 16 * self.duse[q][i])
        self._record(tok, reads, writes)
        return tok

    def barrier(self):
        toks = [(e, self.sem[e], self.cnt[e]) for e in self.eng if self.cnt[e] > 0]
        for q in ("sp", "pool"):
            for i in range(self.NDSEM):
                if self.duse[q][i] > 0:
                    toks.append(("d_%s_%d" % (q, i), self.dsem[q][i], 16 * self.duse[q][i]))
        for e in self.eng:
            for t in toks:
                self._wait(e, t)

    def drain(self, q):
        for i in range(self.NDSEM):
            if self.duse[q][i] > 0:
                self._wait(q, ("d_%s_%d" % (q, i), self.dsem[q][i], 16 * self.duse[q][i]))


class Ring:
    def __init__(self, name, n):
        self.name, self.n, self.i = name, n, 0

    def next(self):
        i = self.i
        self.i = (i + 1) % self.n
        return i, (self.name, i)


class Bump:
    def __init__(self, nc, base):
        self.nc, self.off, self.n = nc, base, 0

    def __call__(self, name, shape, dtype, at=None):
        nbytes = int(np.prod(shape[1:])) * (2 if dtype == BF16 else 4)
        if at is None:
            at = self.off
            self.off = (at + nbytes + 31) // 32 * 32
        assert at + nbytes <= 229376, (name, at, nbytes)
        self.n += 1
        return self.nc.alloc_sbuf_tensor_at("%s_%d" % (name, self.n), list(shape), dtype, offset=at)


def build_program():
    nc = bass.Bass("TRN2", target_bir_lowering=False)
    dt = lambda name, shape, kind="ExternalInput": nc.dram_tensor(name, shape, F32, kind=kind).ap()
    xT = dt("xT", [D, L])
    xoT = dt("xoT", [D, TOWN])
    xhT = dt("xhT", [D, 16])
    w_in = dt("w_in", [D, IN_COLS])
    w_ao = dt("w_ao", [1024, D])
    w_co = dt("w_co", [1024, D])
    w_o = dt("w_o", [D, D])
    w_up = dt("w_up", [D, DFF])
    w_down = dt("w_down", [DFF, D])
    cst_d = dt("cst", [128, NCST])
    cmat_d = dt("cmat", [128, NMAT])
    yT = dt("yT", [D, TOWN], kind="ExternalOutput")

    with ExitStack() as es:
        S = Sched(nc, es)
        ps = lambda name, shape, dtype, st: st.enter_context(nc.psum_tensor(name, shape, dtype))

        G = Bump(nc, 16640)
        cst = G("cst", [128, NCST], F32)
        cmat = G("cmat", [128, NMAT], F32)
        ones_bf = G("ones_bf", [128, 128], BF16)
        id_bf = G("id_bf", [128, 128], BF16)
        tri_bf = G("tri_bf", [128, 4, 128], BF16)
        stage = G("stage", [128, 6, 512], F32)
        sq = G("sq", [128, 4, 512], BF16)
        P0 = 16640 + 23552
        assert G.off <= P0, G.off
        stage_r, sq_r = Ring("stg", 6), Ring("sq", 4)
        ones_f = cmat[:, M_ONES:M_ONES + 128]
        U_f = cmat[:, M_U:M_U + 128]
        id_f = cmat[:, M_ID:M_ID + 128]

        S.dma("sp", cst[:], cst_d, writes=["cst"])
        S.dma("sp", cmat[:], cmat_d, writes=["cmat"])
        S.op("dve", lambda e: e.memset(ones_bf[:], 1.0), writes=["ones_bf"])
        S.op("dve", lambda e: e.tensor_copy(out=id_bf[:], in_=id_f), reads=["cmat"], writes=["id_bf"])
        S.op("dve", lambda e: e.tensor_copy(out=tri_bf[:], in_=cmat[:, M_TRI:M_TRI + 512].rearrange("p (a b) -> p a b", a=4)),
             reads=["cmat"], writes=["tri_bf"])

        def cc(col):
            return cst[:, col:col + 1]

        def load_w(dst, key, dram, row0, kc, col0, ncol):
            src = dram[row0:row0 + kc * 128, col0:col0 + ncol].rearrange("(kc p) n -> p kc n", p=128)
            S.dma("pool", dst, src, writes=[key])

        def rstd_from(out_ap, out_key, ssq_ap, ssq_key, n):
            S.op("act", lambda e: e.activation(out=out_ap, in_=ssq_ap, func=AF.Ln, scale=1.0 / n, bias=cc(C_EPS)),
                 reads=[ssq_key, "cst"], writes=[out_key])
            S.op("act", lambda e: e.activation(out=out_ap, in_=out_ap, func=AF.Exp, scale=-0.5),
                 reads=[out_key], writes=[out_key])

        def rms_prep(src, T, xg, xg_key, gcol, rstd, rstd_key, ssq_ps, ssq_key):
            for kc in range(KC):
                si, sk = stage_r.next()
                S.dma("sp", stage[:, si, :T], src[kc * 128:(kc + 1) * 128, :], writes=[sk])
                qi, qk = sq_r.next()
                S.op("act", lambda e, si=si, qi=qi: e.activation(out=sq[:, qi, :T], in_=stage[:, si, :T], func=AF.Square),
                     reads=[sk], writes=[qk])
                S.op("pe", lambda e, qi=qi, kc=kc: e.matmul(ssq_ps[:, :T], lhsT=ones_bf[:], rhs=sq[:, qi, :T],
                                                            start=(kc == 0), stop=(kc == KC - 1)),
                     reads=[qk, "ones_bf"], writes=[ssq_key])
                S.op("dve", lambda e, si=si, kc=kc: e.tensor_scalar(out=xg[:, kc, :T], in0=stage[:, si, :T],
                                                                    scalar1=cc(gcol + kc), scalar2=None, op0=ALU.mult),
                     reads=[sk, "cst"], writes=[xg_key])
            rstd_from(rstd[:, :T], rstd_key, ssq_ps[:, :T], ssq_key, D)

        with ExitStack() as esb:
            pb = [ps("pb%d" % i, [128, 512], F32, esb) for i in range(8)]
            pk = [("ps", i) for i in range(8)]
            proj_r = Ring("projb", 2)
            s_r = Ring("sbank", 3)
            o_r = Ring("oslot", 2)

            M = Bump(nc, P0)
            QT = M("QT", [128, H, TOWN], BF16)
            wk = M("wk", [128, KC, 1024], BF16)
            wv = M("wv", [128, KC, 1024], BF16)
            wfg = M("wfg", [128, KC, 8], BF16)
            xg = M("xg", [128, KC, 512], BF16)
            rstd_g = M("rstd_g", [128, 512], F32)
            KT2 = M("KT", [128, 2, H, 512], BF16)
            Vg2 = M("Vg", [128, 2, 4, H, 129], BF16)
            PT = M("PT", [128, 4, 4, 128], BF16)
            pt_r = Ring("pt", 4)
            small = M("small", [128, 704], F32)
            scr = M("scr", [128, 6, 512], F32)
            scr_r = Ring("scr", 6)
            acc = M("acc", [128, NR, H, 129], F32)
            rstd_gB = M("rstd_gB", [128, 512], F32)
            rcol = small[:, 0:4]
            fgv = small[:, 8:40]
            e1 = small[:, 40:72]
            spv = small[:, 72:104]
            tot = small[:, 136:168]
            carry = [small[:, 168:176], small[:, 176:184]]
            dlt = small[:, 256:264]
            dltd = small[:, 264:272]
            Sk2 = [small[:, 104:136], small[:, 288:320]]
            sref2 = [small[:, 184:192], small[:, 320:328]]
            bias_g2 = [small[:, 192:224], small[:, 328:360]]
            bias_d2 = [small[:, 224:256], small[:, 360:392]]
            dec2 = [small[:, 272:280], small[:, 392:400]]
            decd2 = [small[:, 280:288], small[:, 400:408]]

            def head_norm(psum_ap, psum_key, T, rstd_ap, rstd_key, gcol, out_ap, out_key, ssq_ps, ssq_key):
                ri, rk = scr_r.next()
                raw = scr[:, ri, :T]
                S.op("dve", lambda e: e.tensor_tensor(out=raw, in0=psum_ap, in1=rstd_ap, op=ALU.mult),
                     reads=[psum_key, rstd_key], writes=[rk])
                qi, qk = sq_r.next()
                S.op("act", lambda e: e.activation(out=sq[:, qi, :T], in_=raw, func=AF.Square), reads=[rk], writes=[qk])
                S.op("pe", lambda e: e.matmul(ssq_ps[:, :T], lhsT=ones_bf[:], rhs=sq[:, qi, :T], start=True, stop=True),
                     reads=[qk, "ones_bf"], writes=[ssq_key])
                r2i, r2k = scr_r.next()
                rn = scr[:, r2i, :T]
                rstd_from(rn, r2k, ssq_ps[:, :T], ssq_key, DH)
                S.op("dve", lambda e: e.scalar_tensor_tensor(out=out_ap, in0=raw, scalar=cc(gcol), in1=rn,
                                                             op0=ALU.mult, op1=ALU.mult),
                     reads=[rk, r2k, "cst"], writes=[out_key])

            S.op("dve", lambda e: e.memset(Vg2[:, :, :, :, 128:129], 1.0), writes=["Vg1"])
            S.op("dve", lambda e: e.memset(carry[0], 0.0), writes=["carry0"])

            for i in range(4):
                load_w(wv[:, :, i * 256:(i + 1) * 256], ("wv", i), w_in, 0, KC, OFF_Q + i * 256, 256)
            for i in range(4):
                load_w(wk[:, :, i * 256:(i + 1) * 256], ("wk", i), w_in, 0, KC, OFF_K + i * 256, 256)
            load_w(wfg[:], "wfg", w_in, 0, KC, OFF_FG, 8)

            NOP = lambda: None
            misc = pb[2]
            xg_d, rstd_g_d = xg, rstd_g
            xgB = KT2[:].rearrange("p a h t -> p (a h) t")
            xgB_key = ("multi", ("KT", 0), ("KT", 1))

            def prep_thunks_src(src_, T, xg=None, xg_key="xg", rstd_g=None, rstd_key="rstd", sb_=2):
                xg = xg_d if xg is None else xg
                rstd_g = rstd_g_d if rstd_g is None else rstd_g
                th = []
                st = {}

                def p1(kc):
                    si, sk = stage_r.next()
                    S.dma("sp", stage[:, si, :T], src_[kc * 128:(kc + 1) * 128, :], writes=[sk])
                    qi, qk = sq_r.next()
                    S.op("dve", lambda e: e.tensor_tensor(out=sq[:, qi, :T], in0=stage[:, si, :T], in1=stage[:, si, :T], op=ALU.mult),
                         reads=[sk], writes=[qk])
                    S.op("dve", lambda e: e.tensor_scalar(out=xg[:, kc, :T], in0=stage[:, si, :T],
                                                          scalar1=cc(C_GMIX + kc), scalar2=None, op0=ALU.mult),
                         reads=[sk, "cst"], writes=[xg_key])
                    st[kc] = (qi, qk)

                def p2(kc):
                    qi, qk = st[kc]
                    S.op("pe", lambda e: e.matmul(pb[sb_][:, :T], lhsT=ones_bf[:], rhs=sq[:, qi, :T],
                                                  start=(kc == 0), stop=(kc == KC - 1)),
                         reads=[qk, "ones_bf"], writes=[pk[sb_]])

                for s in range(KC + 2):
                    def f(s=s):
                        if s < KC:
                            p1(s)
                        if s >= 2:
                            p2(s - 2)
                    th.append(f)
                th.append(NOP)
                th.append(lambda: rstd_from(rstd_g[:, :T], rstd_key, pb[sb_][:, :T], pk[sb_], D))
                return th

            def norm_item(wbuf, wname, h, gcol, out_ap, out_key, T, xg=None, xg_key="xg", rstd_g=None, rstd_key="rstd"):
                xg = xg_d if xg is None else xg
                rstd_g = rstd_g_d if rstd_g is None else rstd_g
                st = {}
                def sA(half):
                    if half == 0:
                        bi, bk = proj_r.next()
                        st["bank"] = (pb[bi], pk[bi])
                    bank, bkey = st["bank"]
                    def mm(e):
                        for kc in range(half * 8, half * 8 + 8):
                            ins = e.matmul(bank[:, :T], lhsT=wbuf[:, kc, h * 128:(h + 1) * 128], rhs=xg[:, kc, :T],
                                           start=(kc == 0), stop=(kc == KC - 1))
                        return ins
                    S.op("pe", mm, reads=[(wname, h // 2), xg_key], writes=[bkey])
                def sB1():
                    bank, bkey = st["bank"]
                    ri, rk = scr_r.next()
                    raw = scr[:, ri, :T]
                    S.op("dve", lambda e: e.tensor_tensor(out=raw, in0=bank[:, :T], in1=rstd_g[:, :T], op=ALU.mult),
                         reads=[bkey, rstd_key], writes=[rk])
                    st["raw"] = (raw, rk)
                def sB2():
                    raw, rk = st["raw"]
                    qi, qk = sq_r.next()
                    S.op("dve", lambda e: e.tensor_tensor(out=sq[:, qi, :T], in0=raw, in1=raw, op=ALU.mult), reads=[rk], writes=[qk])
                    st["sq"] = (qi, qk)
                def sC1():
                    qi, qk = st["sq"]
                    S.op("pe", lambda e: e.matmul(pb[2][:, :T], lhsT=ones_bf[:], rhs=sq[:, qi, :T], start=True, stop=True),
                         reads=[qk, "ones_bf"], writes=[pk[2]])
                def sC1b():
                    r2i, r2k = scr_r.next()
                    rn = scr[:, r2i, :T]
                    S.op("dve", lambda e: e.tensor_copy(out=rn, in_=pb[2][:, :T]), reads=[pk[2]], writes=[r2k])
                    st["rn"] = (rn, r2k)
                def sC2():
                    rn, r2k = st["rn"]
                    rstd_from(rn, r2k, rn, r2k, DH)
                def sD():
                    raw, rk = st["raw"]
                    rn, r2k = st["rn"]
                    S.op("dve", lambda e: e.scalar_tensor_tensor(out=out_ap, in0=raw, scalar=cc(gcol), in1=rn,
                                                                 op0=ALU.mult, op1=ALU.mult),
                         reads=[rk, r2k, "cst"], writes=[out_key])
                return [lambda: sA(0), lambda: sA(1), NOP, sB1, sB2, sC1, NOP, sC1b, sC2, sD]

            def pipeline(items):
                th = []
                nit = len(items)
                for s in range(2 * nit + 10):
                    def f(s=s):
                        for i in range(nit):
                            d = s - 2 * i
                            if 0 <= d < len(items[i]):
                                items[i][d]()
                    th.append(f)
                return th

            for f in prep_thunks_src(xoT[:, 0:512], 512):
                f()
            pa = pipeline([norm_item(wv, "wv", h, C_GQ, QT[:, h, 0:512], ("QT", 0), 512) for h in range(H)])
            pb_ = prep_thunks_src(xoT[:, 512:1024], 512, xgB, xgB_key, rstd_gB, "rstdB", 3)
            for i in range(max(len(pa), len(pb_))):
                if i < len(pa):
                    pa[i]()
                if i < len(pb_):
                    pb_[i]()
            for f in pipeline([norm_item(wv, "wv", h, C_GQ, QT[:, h, 512:1024], ("QT", 1), 512, xgB, xgB_key, rstd_gB, "rstdB")
                               for h in range(H)]):
                f()
            for i in range(4):
                load_w(wv[:, :, i * 256:(i + 1) * 256], ("wv", i), w_in, 0, KC, OFF_V + i * 256, 256)

            groups = [(0, 1, NMETA)] + [(NMETA + 512 * g, 4, 128) for g in range(8)]
            def prep_thunks(g):
                tok0, nb, bt = groups[g]
                return prep_thunks_src(xT[:, tok0:tok0 + nb * bt], nb * bt)

            def prep(g):
                for f in prep_thunks(g):
                    f()

            def proj_thunks(gi):
                tok0, nb, bt = groups[gi]
                T = nb * bt
                par = gi % 2
                KT, Vg = KT2[:, par], Vg2[:, par]
                kKT, kVg = ("KT", par), ("Vg", par)
                Sk, sref, bias_g, bias_d, dec, decd = Sk2[par], sref2[par], bias_g2[par], bias_d2[par], dec2[par], decd2[par]
                kS = lambda n: (n, par)
                c_old, c_new = carry[par], carry[1 - par]
                ko, kn = "carry%d" % par, "carry%d" % (1 - par)

                def k_item(h):
                    return norm_item(wk, "wk", h, C_GK, KT[:, h, :T], kKT, T)

                def v_item(j, ch):
                    st = {}
                    def sA(half):
                        if half == 0:
                            bi, bk = proj_r.next()
                            st["bank"] = (pb[bi], pk[bi])
                        bank, bkey = st["bank"]
                        def mm(e):
                            for kc in range(half * 8, half * 8 + 8):
                                ins = e.matmul(bank[:bt, :], lhsT=xg[:, kc, j * bt:(j + 1) * bt],
                                               rhs=wv[:, kc, ch * 512:(ch + 1) * 512],
                                               start=(kc == 0), stop=(kc == KC - 1))
                            return ins
                        S.op("pe", mm, reads=[("wv", 2 * ch), ("wv", 2 * ch + 1), "xg"], writes=[bkey])
                    def sB():
                        bank, bkey = st["bank"]
                        S.op("dve", lambda e: e.tensor_scalar(
                            out=Vg[:bt, j, ch * 4:(ch + 1) * 4, 0:128],
                            in0=bank[:bt, :].rearrange("p (a b) -> p a b", a=4),
                            scalar1=rcol[:bt, j:j + 1], scalar2=None, op0=ALU.mult),
                            reads=[bkey, "rcol"], writes=[kVg])
                    return [lambda: sA(0), lambda: sA(1), NOP, sB]

                def rcolA():
                    def mm_rcol(e):
                        for j in range(nb):
                            ins = e.matmul(misc[:bt, 256 + j:257 + j], lhsT=rstd_g[:, j * bt:(j + 1) * bt], rhs=id_f[:, 0:1],
                                           start=True, stop=True)
                        return ins
                    S.op("pe", mm_rcol, reads=["rstd", "cmat"], writes=[pk[2]])
                def rcolB():
                    S.op("dve", lambda e: e.tensor_copy(out=rcol[:bt, 0:nb], in_=misc[:bt, 256:256 + nb]),
                         reads=[pk[2]], writes=["rcol"])

                def g1():
                    def mm_fg(e):
                        for j in range(nb):
                            for kc in range(KC):
                                ins = e.matmul(misc[:bt, j * 8:(j + 1) * 8], lhsT=xg[:, kc, j * bt:(j + 1) * bt],
                                               rhs=wfg[:, kc, :], start=(kc == 0), stop=(kc == KC - 1))
                        return ins
                    S.op("pe", mm_fg, reads=["wfg", "xg"], writes=[pk[2]])
                def g2():
                    for j in range(nb):
                        S.op("dve", lambda e, j=j: e.scalar_tensor_tensor(
                            out=fgv[:bt, j * 8:(j + 1) * 8], in0=misc[:bt, j * 8:(j + 1) * 8], scalar=rcol[:bt, j:j + 1],
                            in1=cst[:bt, C_BFG:C_BFG + 8], op0=ALU.mult, op1=ALU.add),
                            reads=[pk[2], "rcol", "cst"], writes=["fgv"])
                def g3():
                    S.op("act", lambda e: e.activation(out=e1[:bt, :nb * 8], in_=fgv[:bt, :nb * 8], func=AF.Exp, scale=-1.0),
                         reads=["fgv"], writes=["e1"])
                    S.op("act", lambda e: e.activation(out=spv[:bt, :nb * 8], in_=e1[:bt, :nb * 8], func=AF.Ln, bias=cst[:bt, C_ONE:C_ONE + 1]),
                         reads=["e1", "cst"], writes=["spv"])
                def g4():
                    def mm_cum(e):
                        for j in range(nb):
                            ins = e.matmul(misc[:bt, 64 + j * 8:64 + (j + 1) * 8], lhsT=U_f[:bt, :bt], rhs=spv[:bt, j * 8:(j + 1) * 8],
                                           start=True, stop=(j == 0))
                            for i in range(j):
                                ins = e.matmul(misc[:bt, 64 + j * 8:64 + (j + 1) * 8], lhsT=ones_f[:bt, :bt], rhs=spv[:bt, i * 8:(i + 1) * 8],
                                               start=False, stop=(i == j - 1))
                        for j in range(nb):
                            ins = e.matmul(misc[:, 128 + j * 8:128 + (j + 1) * 8], lhsT=ones_f[:bt, :], rhs=spv[:bt, j * 8:(j + 1) * 8],
                                           start=True, stop=True)
                        return ins
                    S.op("pe", mm_cum, reads=["spv", "cmat"], writes=[pk[2]])
                def g5():
                    S.op("dve", lambda e: e.tensor_copy(out=tot[:, :nb * 8], in_=misc[:, 128:128 + nb * 8]),
                         reads=[pk[2]], writes=["tot"])
                    for j in range(nb):
                        S.op("dve", lambda e, j=j: e.tensor_tensor(
                            out=Sk[:bt, j * 8:(j + 1) * 8], in0=misc[:bt, 64 + j * 8:64 + (j + 1) * 8], in1=c_old[:bt], op=ALU.add),
                            reads=[pk[2], ko], writes=[kS("Sk")])
                    for j in range(nb):
                        src_ = c_old if j == 0 else c_new
                        S.op("dve", lambda e, j=j, src_=src_: e.tensor_tensor(out=c_new, in0=src_, in1=tot[:, j * 8:(j + 1) * 8], op=ALU.add),
                             reads=["tot", ko, kn], writes=[kn])
                    S.op("dve", lambda e: e.tensor_tensor(out=dlt, in0=c_new, in1=c_old, op=ALU.subtract),
                         reads=[ko, kn], writes=["dlt"])
                    for j in range(nb):
                        S.op("dve", lambda e, j=j: e.tensor_tensor(
                            out=bias_g[:bt, j * 8:(j + 1) * 8], in0=Sk[:bt, j * 8:(j + 1) * 8], in1=c_new[:bt], op=ALU.subtract),
                            reads=[kS("Sk"), kn], writes=[kS("bias_g")])
                    if gi >= 1:
                        for j in range(nb):
                            src_ = c_old if j == 0 else sref
                            S.op("dve", lambda e, j=j, src_=src_: e.scalar_tensor_tensor(
                                out=sref, in0=tot[:, j * 8:(j + 1) * 8], scalar=cc(C_PRESEL + j), in1=src_, op0=ALU.mult, op1=ALU.add),
                                reads=["tot", ko, kS("sref"), "cst"], writes=[kS("sref")])
                        S.op("dve", lambda e: e.tensor_tensor(out=dltd, in0=sref, in1=c_old, op=ALU.subtract),
                             reads=[kS("sref"), ko], writes=["dltd"])
                        for j in range(nb):
                            S.op("dve", lambda e, j=j: e.scalar_tensor_tensor(
                                out=bias_d[:, j * 8:(j + 1) * 8], in0=Sk[:, j * 8:(j + 1) * 8], scalar=cc(C_MASKB + j), in1=sref,
                                op0=ALU.add, op1=ALU.subtract), reads=[kS("Sk"), kS("sref"), "cst"], writes=[kS("bias_d")])
                def g6():
                    S.op("act", lambda e: e.activation(out=dec, in_=dlt, func=AF.Exp, scale=-1.0), reads=["dlt"], writes=[kS("dec")])
                    if gi >= 1:
                        S.op("act", lambda e: e.activation(out=decd, in_=dltd, func=AF.Exp, scale=-1.0), reads=["dltd"], writes=[kS("decd")])

                items = [[rcolA, NOP, NOP, rcolB]] + [k_item(h) for h in range(H)] + [v_item(j, ch) for j in range(nb) for ch in range(2)]
                th = pipeline(items)
                th += [g1, NOP, g2, g3, NOP, g4, NOP, g5, g6]
                return th

            PTf = PT[:].rearrange("p a b c -> p a (b c)")

            def attn_units(gi):
                tok0, nb, bt = groups[gi]
                par = gi % 2
                KT, Vg = KT2[:, par], Vg2[:, par]
                kKT, kVg = ("KT", par), ("Vg", par)
                if gi == 0:
                    trips = [(0, 3, False), (3, 3, False), (6, 2, False)]
                else:
                    trips = [(gi - 1, 1, True)]
                    r = gi
                    while r < NR:
                        n = min(3, NR - r)
                        trips.append((r, n, False))
                        r += n
                out = []
                for h in range(H):
                    for (r0, size, diag) in trips:
                        ust = {}
                        bias_t, bias_k = (bias_d2[par], ("bias_d", par)) if diag else (bias_g2[par], ("bias_g", par))
                        dec_t, dec_k = (decd2[par], ("decd", par)) if diag else (dec2[par], ("dec", par))
                        W = 128 * size
                        for j in range(nb):
                            st = {}

                            def e_s(h=h, r0=r0, W=W, j=j, st=st):
                                si, _ = s_r.next()
                                sbank, skey = pb[3 + si], pk[3 + si]
                                S.op("pe", lambda e: e.matmul(sbank[:bt, 0:W], lhsT=KT[:, h, j * bt:(j + 1) * bt],
                                                              rhs=QT[:, h, r0 * 128:r0 * 128 + W], start=True, stop=True),
                                     reads=[kKT, ("QT", 0), ("QT", 1)], writes=[skey])
                                st["s"] = (sbank, skey)

                            def e_e(h=h, W=W, j=j, st=st, diag=diag, bias_t=bias_t, bias_k=bias_k):
                                sbank, skey = st["s"]
                                pi, pkk = pt_r.next()
                                st["p"] = (pi, pkk)
                                S.op("act", lambda e: e.activation(
                                    out=PTf[:bt, pi, 0:W], in_=sbank[:bt, 0:W], func=AF.Exp,
                                    bias=bias_t[:bt, j * 8 + h:j * 8 + h + 1], scale=SCALE),
                                    reads=[skey, bias_k], writes=[pkk])
                                if diag:
                                    S.op("dve", lambda e: e.tensor_tensor(out=PTf[:, pi, 0:128], in0=PTf[:, pi, 0:128], in1=tri_bf[:, j, :], op=ALU.mult),
                                         reads=[pkk, "tri_bf"], writes=[pkk])

                            def e_pv(h=h, size=size, j=j, st=st, ust=ust):
                                pi, pkk = st["p"]
                                if j == 0:
                                    oi, _ = o_r.next()
                                    ust["o"] = (pb[6 + oi], pk[6 + oi])
                                obank, okey = ust["o"]
                                def mm(e):
                                    for k in range(size):
                                        ins = e.matmul(obank[:, k * 129:(k + 1) * 129], lhsT=PTf[:bt, pi, k * 128:(k + 1) * 128],
                                                       rhs=Vg[:bt, j, h, :], start=(j == 0 and k == 0),
                                                       stop=(j == nb - 1 and k == size - 1))
                                    return ins
                                S.op("pe", mm, reads=[pkk, kVg, "Vg1"], writes=[okey])

                            def e_acc(h=h, r0=r0, size=size, j=j, ust=ust, dec_t=dec_t, dec_k=dec_k):
                                if j != nb - 1:
                                    return
                                obank, okey = ust["o"]
                                akeys = [("acc", r0 + k, h) for k in range(size)]
                                oview = obank[:, 0:size * 129].rearrange("p (k c) -> p k c", k=size)
                                aview = acc[:, r0:r0 + size, h, :]
                                if gi == 0:
                                    S.op("dve", lambda e: e.tensor_copy(out=aview, in_=oview), reads=[okey], writes=akeys)
                                else:
                                    S.op("dve", lambda e: e.scalar_tensor_tensor(
                                        out=aview, in0=aview, scalar=dec_t[:, h:h + 1], in1=oview,
                                        op0=ALU.mult, op1=ALU.add), reads=[okey, dec_k] + akeys, writes=akeys)
                            out.append((e_s, e_e, e_pv, e_acc))
                return out

            prep(0)
            for f in proj_thunks(0):
                f()
            NG = len(groups)
            all_units, sched = [], []
            thunks = {}
            for gi in range(NG):
                units = attn_units(gi)
                pth = proj_thunks(gi + 1) if gi + 1 < NG else []
                if gi == 0:
                    pth = prep_thunks(1) + pth
                if gi + 2 < NG:
                    pth = pth + prep_thunks(gi + 2)
                thunks[gi] = [pth, 0]
                for li in range(len(units)):
                    sched.append((gi, li, len(units)))
                all_units += units
            NU = len(all_units)
            OFFS = (0, 2, 3, 4)
            for i in range(NU + OFFS[-1]):
                for k in range(4):
                    u = i - OFFS[k]
                    if 0 <= u < NU:
                        all_units[u][k]()
                if i < NU:
                    gi, li, nu = sched[i]
                    pth, pj = thunks[gi]
                    npj = len(pth)
                    while pj < npj and ((pj + 1) * nu <= (li + 1) * npj or li == nu - 1):
                        pth[pj]()
                        pj += 1
                    thunks[gi][1] = pj

        S.barrier()
        H0 = P0 + 32768
        Z0 = P0 + 147456
        xog = Bump(nc, P0)("xog", [128, KC, TOWN], BF16)
        h1 = Bump(nc, H0)("h1", [128, KC, TOWN], F32)
        HB = Bump(nc, H0)
        aT = HB("aT", [128, H, TOWN], BF16)
        cT = HB("cT", [128, 8, TOWN], BF16)
        ubuf = HB("ubuf", [128, NR, 130], F32)
        ybuf = HB("ybuf", [128, NR, 128], F32)
        xhg = HB("xhg", [128, KC, 16], BF16)
        rstd_h = HB("rstd_h", [128, 16], F32)
        rstd2_h = HB("rstd2_h", [128, 16], F32)
        atok = HB("atok", [128, 8, 128], BF16)
        recs = HB("recs", [128, NR * H], F32)
        tmpA = HB("tmpA", [128, 4, 512], F32)
        assert HB.off <= H0 + 65536
        CB_ = Bump(nc, P0 + 98304)
        NWS = 10
        wring = CB_("wring", [128, NWS * 2048], BF16)
        rstd_o = CB_("rstd_o", [128, TOWN], F32)
        rstd2_o = CB_("rstd2_o", [128, TOWN], F32)
        assert CB_.off <= Z0
        ZB = Bump(nc, Z0)
        mT = ZB("mT", [128, KC, TOWN], BF16)
        tmp = ZB("tmp", [128, 4, 512], F32)
        uT = mT
        hng = xog
        atok_r = Ring("atok", 8)
        w_r = Ring("w", NWS)
        tmp_r = Ring("tmp", 4)
        with ExitStack() as esc:
            pc = [ps("pc%d" % i, [128, 512], F32, esc) for i in range(8)]
            pck = [("pc", i) for i in range(8)]
            TS = [slice(0, 512), slice(512, 1024)]

            def wtile(dram, row0, kc, col0, ncol):
                nsl = (kc * ncol * 2 + 4095) // 4096
                if w_r.i + nsl > NWS:
                    w_r.i = 0
                i0 = w_r.i
                keys = [w_r.next()[1] for _ in range(nsl)]
                view = wring[:, i0 * 2048:i0 * 2048 + kc * ncol].rearrange("p (k n) -> p k n", k=kc)
                src_ = dram[row0:row0 + kc * 128, col0:col0 + ncol].rearrange("(kc p) n -> p kc n", p=128)
                S.dma("pool", view, src_, writes=keys)
                return view, ("multi",) + tuple(keys)

            for t in range(2):
                rms_prep(xoT[:, TS[t]], 512, xog[:, :, TS[t]], ("xog", t), C_GMIX, rstd_o[:, TS[t]], ("rstd_o", t), pc[7], pck[7])
                S.op("dve", lambda e, t=t: e.tensor_tensor(out=rstd2_o[:, TS[t]], in0=rstd_o[:, TS[t]], in1=rstd_o[:, TS[t]], op=ALU.mult),
                     reads=[("rstd_o", t)], writes=[("rstd2_o", t)])
            rms_prep(xhT[:, :], 16, xhg, "xhg", C_GMIX, rstd_h, "rstd_h", pc[7], pck[7])
            S.op("dve", lambda e: e.tensor_tensor(out=rstd2_h[:], in0=rstd_h[:], in1=rstd_h[:], op=ALU.mult),
                 reads=["rstd_h"], writes=["rstd2_h"])

            tmp_mlp, tmp = tmp, tmpA
            S.op("dve", lambda e: e.reciprocal(out=recs[:].rearrange("p (r h) -> p r h", r=NR), in_=acc[:, :, :, 128]),
                 writes=["recs"])
            c0_chunks = [(h, rh) for h in range(H) for rh in range(2)]
            pc7_bf = pc[7][:, :].bitcast(BF16)
            c0_state = {}

            def c0_a(ci):
                h, rh = c0_chunks[ci]
                slots = []
                for r4 in range(4):
                    r = rh * 4 + r4
                    ai, ak = atok_r.next()
                    S.op("dve", lambda e, r=r, ai=ai: e.tensor_scalar(
                        out=atok[:, ai, :], in0=acc[:, r, h, 0:128], scalar1=recs[:, r * H + h:r * H + h + 1],
                        scalar2=None, op0=ALU.mult), reads=["recs"], writes=[ak])
                    slots.append((ai, ak))
                c0_state[ci] = slots

            def c0_b(ci):
                for r4, (ai, ak) in enumerate(c0_state[ci]):
                    S.op("pe", lambda e, ai=ai, r4=r4: e.transpose(pc7_bf[:, r4 * 128:(r4 + 1) * 128], atok[:, ai, :], id_bf[:]),
                         reads=[ak, "id_bf"], writes=[pck[7]])

            def c0_c(ci):
                h, rh = c0_chunks[ci]
                S.op("act", lambda e: e.copy(out=aT[:, h, rh * 512:(rh + 1) * 512], in_=pc7_bf[:, 0:512]),
                     reads=[pck[7]], writes=[("aT", rh)])

            def c0_step(i):
                if 0 <= i - 2 < 16:
                    c0_c(i - 2)
                if 0 <= i - 1 < 16:
                    c0_b(i - 1)
                if 0 <= i < 16:
                    c0_a(i)

            unit = 0
            for j in range(8):
                wcb, kcb = wtile(w_in, 0, KC, OFF_CB + j * 128, 128)
                wcc, kcc = wtile(w_in, 0, KC, OFF_CC + j * 128, 128)
                wcx, kcx = wtile(w_in, 0, KC, OFF_CX + j * 128, 128)
                for jj in range(1):
                    cs = slice(0, 128)
                    hb = 6
                    def mm_h(e, cs=cs, wcc=wcc, wcx=wcx, hb=hb):
                        for kc in range(KC):
                            ins = e.matmul(pc[hb][:, 0:16], lhsT=wcc[:, kc, cs], rhs=xhg[:, kc, :], start=(kc == 0), stop=(kc == KC - 1))
                        for kc in range(KC):
                            ins = e.matmul(pc[hb][:, 16:32], lhsT=wcx[:, kc, cs], rhs=xhg[:, kc, :], start=(kc == 0), stop=(kc == KC - 1))
                        return ins
                    S.op("pe", mm_h, reads=[kcc, kcx, "xhg"], writes=[pck[hb]])
                    ti, tk = tmp_r.next()
                    S.op("dve", lambda e, ti=ti, hb=hb: e.tensor_tensor(out=tmp[:, ti, 0:16], in0=pc[hb][:, 0:16], in1=rstd2_h[:], op=ALU.mult),
                         reads=[pck[hb], "rstd2_h"], writes=[tk])
                    S.op("dve", lambda e, ti=ti, hb=hb: e.tensor_tensor(out=ubuf[:, :, 0:2], in0=tmp[:, ti, 0:16].rearrange("p (r c) -> p r c", c=2),
                                                                        in1=pc[hb][:, 16:32].rearrange("p (r c) -> p r c", c=2), op=ALU.mult),
                         reads=[tk, pck[hb]], writes=[("ubuf", 0), ("ubuf", 1)])
                    for t in range(2):
                        b0 = 3 * (unit % 2)
                        unit += 1
                        rs = slice(t * 4, (t + 1) * 4)
                        def mm3(e, cs=cs, wcb=wcb, wcc=wcc, wcx=wcx, b0=b0, t=t):
                            for bank, w in ((pc[b0], wcc), (pc[b0 + 1], wcx), (pc[b0 + 2], wcb)):
                                for kc in range(KC):
                                    ins = e.matmul(bank[:, :], lhsT=w[:, kc, cs], rhs=xog[:, kc, TS[t]], start=(kc == 0), stop=(kc == KC - 1))
                            return ins
                        S.op("pe", mm3, reads=[kcb, kcc, kcx, ("xog", t)], writes=[pck[b0], pck[b0 + 1], pck[b0 + 2]])
                        c0_step(unit - 1)
                        ti, tk = tmp_r.next()
                        S.op("dve", lambda e, ti=ti, b0=b0, t=t: e.tensor_tensor(out=tmp[:, ti, :], in0=pc[b0][:, :], in1=rstd2_o[:, TS[t]], op=ALU.mult),
                             reads=[pck[b0], ("rstd2_o", t)], writes=[tk])
                        S.op("dve", lambda e, ti=ti, b0=b0, rs=rs: e.tensor_tensor(
                            out=ubuf[:, rs, 2:130], in0=tmp[:, ti, :].rearrange("p (r c) -> p r c", r=4),
                            in1=pc[b0 + 1][:, :].rearrange("p (r c) -> p r c", r=4), op=ALU.mult),
                            reads=[tk, pck[b0 + 1]], writes=[("ubuf", t)])
                        cw = lambda tap, j=j: cc(C_CONV + j * 3 + tap)
                        yk = ("ybuf", t)
                        S.op("dve", lambda e, cw=cw, rs=rs: e.tensor_scalar(out=ybuf[:, rs, :], in0=ubuf[:, rs, 0:128], scalar1=cw(0), scalar2=None, op0=ALU.mult),
                             reads=[("ubuf", t), "cst"], writes=[yk])
                        S.op("dve", lambda e, cw=cw, rs=rs: e.scalar_tensor_tensor(out=ybuf[:, rs, :], in0=ubuf[:, rs, 1:129], scalar=cw(1), in1=ybuf[:, rs, :],
                                                                                   op0=ALU.mult, op1=ALU.add), reads=[("ubuf", t), "cst", yk], writes=[yk])
                        S.op("dve", lambda e, cw=cw, rs=rs: e.scalar_tensor_tensor(out=ybuf[:, rs, :], in0=ubuf[:, rs, 2:130], scalar=cw(2), in1=ybuf[:, rs, :],
                                                                                   op0=ALU.mult, op1=ALU.add), reads=[("ubuf", t), "cst", yk], writes=[yk])
                        ti, tk = tmp_r.next()
                        S.op("dve", lambda e, ti=ti, b0=b0, t=t: e.tensor_tensor(out=tmp[:, ti, :], in0=pc[b0 + 2][:, :], in1=rstd_o[:, TS[t]], op=ALU.mult),
                             reads=[pck[b0 + 2], ("rstd_o", t)], writes=[tk])
                        S.op("dve", lambda e, ti=ti, j=j, t=t, rs=rs: e.tensor_tensor(
                            out=cT[:, j, TS[t]], in0=tmp[:, ti, :], in1=ybuf[:, rs, :].rearrange("p r c -> p (r c)"), op=ALU.mult),
                            reads=[tk, yk], writes=[("cT", t)])

            c0_step(16)
            c0_step(17)

            unit = 0
            for cb in range(16):
                wa, ka = wtile(w_ao, 0, 8, cb * 128, 128)
                wc_, kc_ = wtile(w_co, 0, 8, cb * 128, 128)
                wg0, kg0 = wtile(w_in, 0, KC, OFF_GL + cb * 128, 128)
                wg1, kg1 = wtile(w_in, 0, KC, OFF_GL + D + cb * 128, 128)
                for jj in range(1):
                    cs = slice(0, 128)
                    for t in range(2):
                        b0 = 4 * (unit % 2)
                        unit += 1
                        def mm4(e, cs=cs, t=t, wa=wa, wc_=wc_, wg0=wg0, wg1=wg1, b0=b0):
                            for hh in range(8):
                                ins = e.matmul(pc[b0][:, :], lhsT=wa[:, hh, cs], rhs=aT[:, hh, TS[t]], start=(hh == 0), stop=(hh == 7))
                            for hh in range(8):
                                ins = e.matmul(pc[b0 + 1][:, :], lhsT=wc_[:, hh, cs], rhs=cT[:, hh, TS[t]], start=(hh == 0), stop=(hh == 7))
                            for kc in range(KC):
                                ins = e.matmul(pc[b0 + 2][:, :], lhsT=wg0[:, kc, cs], rhs=xog[:, kc, TS[t]], start=(kc == 0), stop=(kc == KC - 1))
                            for kc in range(KC):
                                ins = e.matmul(pc[b0 + 3][:, :], lhsT=wg1[:, kc, cs], rhs=xog[:, kc, TS[t]], start=(kc == 0), stop=(kc == KC - 1))
                            return ins
                        S.op("pe", mm4, reads=[ka, kc_, kg0, kg1, ("aT", t), ("cT", t), ("xog", t)],
                             writes=[pck[b0], pck[b0 + 1], pck[b0 + 2], pck[b0 + 3]])
                        t0i, t0k = tmp_r.next()
                        t1i, t1k = tmp_r.next()
                        for (ti, tk, bank, boff) in ((t0i, t0k, b0 + 2, 0), (t1i, t1k, b0 + 3, 16)):
                            S.op("dve", lambda e, ti=ti, bank=bank, t=t: e.tensor_tensor(out=tmp[:, ti, :], in0=pc[bank][:, :], in1=rstd_o[:, TS[t]], op=ALU.mult),
                                 reads=[pck[bank], ("rstd_o", t)], writes=[tk])
                            S.op("act", lambda e, ti=ti, cb=cb, boff=boff: e.activation(out=tmp[:, ti, :], in_=tmp[:, ti, :], func=AF.Sigmoid,
                                                                                    bias=cc(C_BGATE + boff + cb)), reads=[tk, "cst"], writes=[tk])
                        S.op("dve", lambda e, t0i=t0i, b0=b0: e.tensor_tensor(out=tmp[:, t0i, :], in0=tmp[:, t0i, :], in1=pc[b0][:, :], op=ALU.mult),
                             reads=[t0k, pck[b0]], writes=[t0k])
                        S.op("dve", lambda e, t1i=t1i, b0=b0: e.tensor_tensor(out=tmp[:, t1i, :], in0=tmp[:, t1i, :], in1=pc[b0 + 1][:, :], op=ALU.mult),
                             reads=[t1k, pck[b0 + 1]], writes=[t1k])
                        S.op("dve", lambda e, t0i=t0i, t1i=t1i, cb=cb, t=t: e.tensor_tensor(out=mT[:, cb, TS[t]], in0=tmp[:, t0i, :], in1=tmp[:, t1i, :], op=ALU.add),
                             reads=[t0k, t1k], writes=[("mT", t)])

            unit = 0
            pend = []
            for c2 in range(8):
                wo, ko_ = wtile(w_o, 0, KC, c2 * 256, 256)
                for jj in range(2):
                    cb = c2 * 2 + jj
                    cs = slice(jj * 128, (jj + 1) * 128)
                    for t in range(2):
                        bank = unit % 4
                        unit += 1
                        def mm(e, cs=cs, wo=wo, bank=bank, t=t):
                            for kc in range(KC):
                                ins = e.matmul(pc[bank][:, :], lhsT=wo[:, kc, cs], rhs=mT[:, kc, TS[t]], start=(kc == 0), stop=(kc == KC - 1))
                            return ins
                        S.op("pe", mm, reads=[ko_, ("mT", t)], writes=[pck[bank]])
                        si, sk = stage_r.next()
                        S.dma("sp", stage[:, si, :], xoT[cb * 128:(cb + 1) * 128, TS[t]], writes=[sk])
                        hk = ("h1", cb, t)
                        S.op("dve", lambda e, si=si, cb=cb, bank=bank, t=t: e.tensor_tensor(out=h1[:, cb, TS[t]], in0=pc[bank][:, :], in1=stage[:, si, :], op=ALU.add),
                             reads=[pck[bank], sk], writes=[hk])
                        qi, qk = sq_r.next()
                        S.op("act", lambda e, qi=qi, cb=cb, t=t: e.activation(out=sq[:, qi, :], in_=h1[:, cb, TS[t]], func=AF.Square),
                             reads=[hk], writes=[qk])
                        pend.append(lambda qi=qi, qk=qk, cb=cb, t=t: S.op(
                            "pe", lambda e: e.matmul(pc[6 + t][:, :], lhsT=ones_bf[:], rhs=sq[:, qi, :], start=(cb == 0), stop=(cb == KC - 1)),
                            reads=[qk, "ones_bf"], writes=[pck[6 + t]]))
                        if len(pend) > 2:
                            pend.pop(0)()
                        S.op("dve", lambda e, cb=cb, t=t: e.tensor_scalar(out=hng[:, cb, TS[t]], in0=h1[:, cb, TS[t]], scalar1=cc(C_GMLP + cb),
                                                                          scalar2=None, op0=ALU.mult), reads=[hk, "cst"], writes=[("xog", t)])
            while pend:
                pend.pop(0)()
            for t in range(2):
                rstd_from(rstd_o[:, TS[t]], ("rstd_o", t), pc[6 + t][:, :], pck[6 + t], D)

            tmp = tmp_mlp
            NQ = 4
            FQ = DFF // NQ
            unit = 0
            for fq in range(NQ):
                for f2 in range(8):
                    wu, ku = wtile(w_up, 0, KC, fq * FQ + f2 * 256, 256)
                    for jj in range(2):
                        fb = f2 * 2 + jj
                        cs = slice(jj * 128, (jj + 1) * 128)
                        for t in range(2):
                            bank = unit % 4
                            unit += 1
                            def mm(e, cs=cs, wu=wu, bank=bank, t=t):
                                for kc in range(KC):
                                    ins = e.matmul(pc[bank][:, :], lhsT=wu[:, kc, cs], rhs=hng[:, kc, TS[t]], start=(kc == 0), stop=(kc == KC - 1))
                                return ins
                            S.op("pe", mm, reads=[ku, ("xog", t)], writes=[pck[bank]])
                            ti, tk = tmp_r.next()
                            S.op("dve", lambda e, ti=ti, bank=bank, t=t: e.scalar_tensor_tensor(
                                out=tmp[:, ti, :], in0=pc[bank][:, :], scalar=0.0, in1=rstd_o[:, TS[t]], op0=ALU.max, op1=ALU.mult),
                                reads=[pck[bank], ("rstd_o", t)], writes=[tk])
                            S.op("act", lambda e, ti=ti, fb=fb, t=t: e.activation(out=uT[:, fb, TS[t]], in_=tmp[:, ti, :], func=AF.Square),
                                 reads=[tk], writes=[("mT", t)])
                for c2 in range(8):
                    wd, kd = wtile(w_down, fq * FQ, KC, c2 * 256, 256)
                    for jj in range(2):
                        cb = c2 * 2 + jj
                        cs = slice(jj * 128, (jj + 1) * 128)
                        for t in range(2):
                            bank = 4 + unit % 4
                            unit += 1
                            def mm(e, wd=wd, bank=bank, cs=cs, t=t):
                                for kc in range(KC):
                                    ins = e.matmul(pc[bank][:, :], lhsT=wd[:, kc, cs], rhs=uT[:, kc, TS[t]], start=(kc == 0), stop=(kc == KC - 1))
                                return ins
                            S.op("pe", mm, reads=[kd, ("mT", t)], writes=[pck[bank]])
                            hk = ("h1", cb, t)
                            S.op("dve", lambda e, cb=cb, bank=bank, t=t: e.tensor_tensor(out=h1[:, cb, TS[t]], in0=h1[:, cb, TS[t]], in1=pc[bank][:, :], op=ALU.add),
                                 reads=[pck[bank], hk], writes=[hk])
                            if fq == NQ - 1:
                                S.dma("sp", yT[cb * 128:(cb + 1) * 128, TS[t]], h1[:, cb, TS[t]], reads=[hk], writes=[("yT", cb, t)])
            S.drain("sp")
            S.drain("pool")
    return nc


_PROGRAM = None


def _consts(p, inp):
    cst = np.zeros((128, NCST), np.float32)
    cst[:, C_GMIX:C_GMIX + 16] = inp["norm_mix"][0].reshape(16, 128).T
    cst[:, C_GMLP:C_GMLP + 16] = inp["norm_mlp"][0].reshape(16, 128).T
    cst[:, C_GQ] = inp["q_norm"][0]
    cst[:, C_GK] = inp["k_norm"][0]
    cst[:, C_BFG:C_BFG + 8] = inp["b_fgate"][0][None, :]
    cst[:, C_BGATE:C_BGATE + 32] = inp["b_gate"][0].reshape(32, 128).T
    cw = inp["conv_w"][0]
    for j in range(8):
        for tap in range(3):
            cst[:, C_CONV + j * 3 + tap] = cw[tap, j * 128:(j + 1) * 128]
    for s in range(4):
        cst[:, C_MASKB + s] = 0.0 if s <= p else NEG
        cst[:, C_PRESEL + s] = 1.0 if s < p else 0.0
    cst[:, C_EPS] = EPS
    cst[:, C_ONE] = 1.0
    cmat = np.zeros((128, NMAT), np.float32)
    k = np.arange(128)[:, None]
    t = np.arange(128)[None, :]
    cmat[:, M_ONES:M_ONES + 128] = 1.0
    cmat[:, M_U:M_U + 128] = (k <= t)
    cmat[:, M_ID:M_ID + 128] = (k == t)
    for s in range(4):
        if s < p:
            m = np.ones((128, 128), np.float32)
        elif s == p:
            m = (k <= t).astype(np.float32)
        else:
            m = np.zeros((128, 128), np.float32)
        cmat[:, M_TRI + s * 128:M_TRI + (s + 1) * 128] = m
    return cst, cmat


def kernel(x, meta_tokens, norm_mix, w_in, b_fgate, b_gate, q_norm, k_norm, conv_w,
           w_attn_out, w_conv_out, w_o, norm_mlp, w_up, w_down):
    global _PROGRAM
    inp = dict(norm_mix=norm_mix, norm_mlp=norm_mlp, q_norm=q_norm, k_norm=k_norm, b_fgate=b_fgate,
               b_gate=b_gate, conv_w=conv_w)
    x = np.asarray(x, np.float32)
    B = x.shape[0]
    shared = dict(
        w_in=np.ascontiguousarray(w_in[0], np.float32), w_ao=np.ascontiguousarray(w_attn_out[0], np.float32),
        w_co=np.ascontiguousarray(w_conv_out[0], np.float32), w_o=np.ascontiguousarray(w_o[0], np.float32),
        w_up=np.ascontiguousarray(w_up[0], np.float32), w_down=np.ascontiguousarray(w_down[0], np.float32))
    in_maps = []
    fullT = []
    for b in range(B):
        full = np.concatenate([np.asarray(meta_tokens, np.float32), x[b]], axis=0)
        fullT.append((full, np.ascontiguousarray(full.T)))
    for c in range(8):
        b, p = c // 4, c % 4
        full, fT = fullT[b]
        own_idx = np.concatenate([np.arange(NMETA + 128 * (4 * r + p), NMETA + 128 * (4 * r + p + 1)) for r in range(NR)])
        halo_idx = np.concatenate([np.array([NMETA + 128 * (4 * r + p) - 2, NMETA + 128 * (4 * r + p) - 1]) for r in range(NR)])
        cst, cmat = _consts(p, inp)
        m = dict(shared)
        m.update(xT=fT, xoT=np.ascontiguousarray(full[own_idx].T), xhT=np.ascontiguousarray(full[halo_idx].T),
                 cst=cst, cmat=cmat)
        in_maps.append(m)
    if _PROGRAM is None:
        _PROGRAM = build_program()
    res = run_bass_kernel_spmd(_PROGRAM, in_maps, core_ids=list(range(8)))
    out = np.empty((B, SEQ, D), np.float32)
    for c in range(8):
        b, p = c // 4, c % 4
        y = res.results[c]["yT"]
        for r in range(NR):
            s0 = 128 * (4 * r + p)
            out[b, s0:s0 + 128, :] = y[:, r * 128:(r + 1) * 128].T
    return out
```
